# Optimizing a Trainium2 kernel written in Bass

```python
import jax, jax.numpy as jnp
from jax import lax
import numpy as np

D_MODEL = 1024
BATCH = 16
SEQ = 4096
DEPTH = 1
DEC_BATCH = 4
DEC_SEQ = 8192
PAST_LEN = 128

D_MIX = D_MODEL
D_ATTN = D_MIX // 2
D_POOL = D_MIX - D_ATTN
HEAD_DIM = 64
N_HEADS = D_ATTN // HEAD_DIM
N_KV_HEADS = 2
GROUP = N_HEADS // N_KV_HEADS
D_Q = N_HEADS * HEAD_DIM
D_KV = N_KV_HEADS * HEAD_DIM
POOL_WINDOWS = (2, 4, 8, 16)
N_POOL_GROUPS = len(POOL_WINDOWS)
POOL_GROUP_DIM = D_POOL // N_POOL_GROUPS
D_IN = D_Q + 2 * D_KV + D_POOL
D_FF = 4 * D_MODEL
GRID_W = 64
ROPE_THETA = 10000.0
Q_BLOCK = 128
EPS = 1e-6
N_MOD = 6

kernel_name = "hymba_attn_pool_encoder"


def _rmsnorm(x, g):
    xf = x.astype(jnp.float32)
    y = xf * lax.rsqrt(jnp.mean(xf * xf, axis=-1, keepdims=True) + EPS) * g.astype(jnp.float32)
    return y.astype(x.dtype)


def _axial_rope_tables(seq_len):
    rows = seq_len // GRID_W
    row_idx = jnp.repeat(jnp.arange(rows, dtype=jnp.float32), GRID_W)
    col_idx = jnp.tile(jnp.arange(GRID_W, dtype=jnp.float32), rows)
    n_freq = HEAD_DIM // 4
    inv_freq = 1.0 / (ROPE_THETA ** (jnp.arange(n_freq, dtype=jnp.float32) / n_freq))
    ang = jnp.concatenate([row_idx[:, None] * inv_freq, col_idx[:, None] * inv_freq], axis=-1)
    return jnp.cos(ang), jnp.sin(ang)


def _apply_rope(x, cos, sin):
    xf = x.astype(jnp.float32)
    x1, x2 = xf[..., :HEAD_DIM // 2], xf[..., HEAD_DIM // 2:]
    c = cos[None, :, None, :]
    s = sin[None, :, None, :]
    return jnp.concatenate([x1 * c - x2 * s, x2 * c + x1 * s], axis=-1).astype(x.dtype)


def _attention(q, k, v):
    b, s = q.shape[0], q.shape[1]
    nblk = s // Q_BLOCK
    qb = q.reshape(b, nblk, Q_BLOCK, N_KV_HEADS, GROUP, HEAD_DIM).transpose(1, 0, 2, 3, 4, 5)
    scale = HEAD_DIM ** -0.5

    def block(qblk):
        sc = jnp.einsum('bqkgd,bskd->bkgqs', qblk, k).astype(jnp.float32) * scale
        p = jax.nn.softmax(sc, axis=-1).astype(v.dtype)
        return jnp.einsum('bkgqs,bskd->bqkgd', p, v)

    ob = lax.map(block, qb)
    return ob.transpose(1, 0, 2, 3, 4, 5).reshape(b, s, D_Q)


def _multiscale_pool(u, w_pool, pool_scale):
    b, s, _ = u.shape
    uf = u.reshape(b, s, N_POOL_GROUPS, POOL_GROUP_DIM).astype(jnp.float32)
    cs = jnp.concatenate([jnp.zeros((b, 1, N_POOL_GROUPS, POOL_GROUP_DIM), jnp.float32),
                          jnp.cumsum(uf, axis=1)], axis=1)
    t = jnp.arange(s)
    pooled = []
    for gi, w in enumerate(POOL_WINDOWS):
        lo = jnp.clip(t - w // 2, 0, s)
        hi = jnp.clip(t - w // 2 + w, 0, s)
        csg = cs[:, :, gi, :]
        cnt = (hi - lo).astype(jnp.float32)[None, :, None]
        pooled.append((csg[:, hi, :] - csg[:, lo, :]) / cnt)
    pooled = jnp.stack(pooled, axis=2)
    mixed = (pooled - uf).astype(u.dtype)
    out = jnp.einsum('bsgc,gcd->bsgd', mixed, w_pool).reshape(b, s, D_POOL)
    return out * pool_scale


def _layer(x, c, w_ada, b_ada, g_pre_mix, g_post_mix, g_pre_mlp, g_post_mlp,
           w_in, g_q, g_k, w_pool, pool_scale, w_out, w_ff1, w_ff2):
    b, s, _ = x.shape
    mod = (jax.nn.silu(c) @ w_ada + b_ada).reshape(b, N_MOD, 1, D_MODEL)
    shift_a, scale_a, gate_a = mod[:, 0], mod[:, 1], mod[:, 2]
    shift_m, scale_m, gate_m = mod[:, 3], mod[:, 4], mod[:, 5]

    h = _rmsnorm(x, g_pre_mix) * (1.0 + scale_a) + shift_a
    proj = h @ w_in
    q = proj[..., :D_Q].reshape(b, s, N_HEADS, HEAD_DIM)
    k = proj[..., D_Q:D_Q + D_KV].reshape(b, s, N_KV_HEADS, HEAD_DIM)
    v = proj[..., D_Q + D_KV:D_Q + 2 * D_KV].reshape(b, s, N_KV_HEADS, HEAD_DIM)
    u = proj[..., D_Q + 2 * D_KV:]
    cos, sin = _axial_rope_tables(s)
    q = _apply_rope(_rmsnorm(q, g_q), cos, sin)
    k = _apply_rope(_rmsnorm(k, g_k), cos, sin)
    a = _attention(q, k, v)
    p = _multiscale_pool(u, w_pool, pool_scale)
    mix = jnp.concatenate([a, p], axis=-1) @ w_out
    x = x + gate_a * _rmsnorm(mix, g_post_mix)

    h = _rmsnorm(x, g_pre_mlp) * (1.0 + scale_m) + shift_m
    f = jnp.square(jax.nn.relu(h @ w_ff1)) @ w_ff2
    x = x + gate_m * _rmsnorm(f, g_post_mlp)
    return x


def setup_inputs(seed: int = 0) -> dict:
    key = jax.random.key(seed)
    ks = jax.random.split(key, 20)
    f32 = jnp.float32

    def nrm(k, shape, scale):
        return jax.random.normal(k, shape, f32) * scale

    def gain(k, shape):
        return 1.0 + 0.02 * jax.random.normal(k, shape, f32)

    return {
        "x_prompt": nrm(ks[0], (BATCH, SEQ, D_MODEL), 1.0),
        "x_sample": nrm(ks[1], (DEC_BATCH, DEC_SEQ, D_MODEL), 1.0),
        "c_prompt": nrm(ks[2], (BATCH, D_MODEL), 1.0),
        "c_sample": nrm(ks[3], (DEC_BATCH, D_MODEL), 1.0),
        "w_ada": nrm(ks[4], (DEPTH, D_MODEL, N_MOD * D_MODEL), 0.5 * D_MODEL ** -0.5),
        "b_ada": nrm(ks[5], (DEPTH, N_MOD * D_MODEL), 0.01),
        "g_pre_mix": gain(ks[6], (DEPTH, D_MODEL)),
        "g_post_mix": gain(ks[7], (DEPTH, D_MODEL)),
        "g_pre_mlp": gain(ks[8], (DEPTH, D_MODEL)),
        "g_post_mlp": gain(ks[9], (DEPTH, D_MODEL)),
        "w_in": nrm(ks[10], (DEPTH, D_MODEL, D_IN), D_MODEL ** -0.5),
        "g_q": gain(ks[11], (DEPTH, HEAD_DIM)),
        "g_k": gain(ks[12], (DEPTH, HEAD_DIM)),
        "w_pool": nrm(ks[13], (DEPTH, N_POOL_GROUPS, POOL_GROUP_DIM, POOL_GROUP_DIM), POOL_GROUP_DIM ** -0.5),
        "pool_scale": gain(ks[14], (DEPTH, D_POOL)),
        "w_out": nrm(ks[15], (DEPTH, D_MIX, D_MODEL), D_MIX ** -0.5),
        "w_ff1": nrm(ks[16], (DEPTH, D_MODEL, D_FF), D_MODEL ** -0.5),
        "w_ff2": nrm(ks[17], (DEPTH, D_FF, D_MODEL), D_FF ** -0.5),
    }


def reference(x_prompt, x_sample, c_prompt, c_sample, w_ada, b_ada, g_pre_mix, g_post_mix,
              g_pre_mlp, g_post_mlp, w_in, g_q, g_k, w_pool, pool_scale, w_out, w_ff1, w_ff2):
    y_prompt = x_prompt
    y_sample = x_sample
    for l in range(DEPTH):
        y_prompt = _layer(y_prompt, c_prompt, w_ada[l], b_ada[l], g_pre_mix[l], g_post_mix[l],
                          g_pre_mlp[l], g_post_mlp[l], w_in[l], g_q[l], g_k[l], w_pool[l],
                          pool_scale[l], w_out[l], w_ff1[l], w_ff2[l])
        y_sample = _layer(y_sample, c_sample, w_ada[l], b_ada[l], g_pre_mix[l], g_post_mix[l],
                          g_pre_mlp[l], g_post_mlp[l], w_in[l], g_q[l], g_k[l], w_pool[l],
                          pool_scale[l], w_out[l], w_ff1[l], w_ff2[l])
    return (y_prompt, y_sample)
```

```python
from contextlib import ExitStack
import itertools
import numpy as np
import concourse.bass as bass
import concourse.mybir as mybir
from concourse.bass_utils import run_bass_kernel_spmd

F32 = mybir.dt.float32
BF16 = mybir.dt.bfloat16
AF = mybir.ActivationFunctionType
ALU = mybir.AluOpType
AX = mybir.AxisListType

D = 1024
DIN = 1280
DFF = 4096
EPS = 1e-6
POOL_W = (2, 4, 8, 16)
GRID_W = 64
ROPE_THETA = 10000.0
N_CORES = 8


class Trk:
    __slots__ = ("w", "r", "sem", "cnt", "name")

    def __init__(self, name=""):
        self.w = None
        self.r = {}
        self.sem = None
        self.cnt = 0
        self.name = name


class Sched:
    def __init__(self, nc):
        self.nc = nc
        self.eng = {}
        for name, h in (("pe", nc.tensor), ("act", nc.scalar), ("dve", nc.vector),
                        ("pool", nc.gpsimd), ("sp", nc.sync)):
            self.eng[name] = dict(h=h, sem=nc.alloc_semaphore("s_" + name), cnt=0, waited={})
        self.dma_trks = []
        self.nsem = 0

    def _deps(self, reads, writes):
        deps = {}
        for t in reads:
            if t.w is not None:
                k, s, v = t.w
                if k not in deps or deps[k][1] < v:
                    deps[k] = (s, v)
        for t in writes:
            if t.w is not None:
                k, s, v = t.w
                if k not in deps or deps[k][1] < v:
                    deps[k] = (s, v)
            for k, (s, v) in t.r.items():
                if k not in deps or deps[k][1] < v:
                    deps[k] = (s, v)
        return deps

    def _wait(self, ename, deps):
        e = self.eng[ename]
        for k, (s, v) in deps.items():
            if k == "pe" and ename == "pe":
                continue
            if e["waited"].get(k, 0) < v:
                e["h"].wait_ge(s, v)
                e["waited"][k] = v

    def op(self, ename, fn, reads=(), writes=()):
        e = self.eng[ename]
        self._wait(ename, self._deps(reads, writes))
        inst = fn(e["h"])
        e["cnt"] += 1
        inst.then_inc(e["sem"], 1)
        rec = (ename, e["sem"], e["cnt"])
        for t in writes:
            t.w = rec
            t.r = {}
        for t in reads:
            t.r[ename] = (e["sem"], e["cnt"])
        return inst

    def dma(self, out, in_, strk, reads=(), writes=()):
        e = self.eng["sp"]
        self._wait("sp", self._deps(reads, writes))
        if strk.sem is None:
            strk.sem = self.nc.alloc_semaphore("d%d" % self.nsem)
            self.nsem += 1
            self.dma_trks.append(strk)
        inst = e["h"].dma_start(out=out, in_=in_)
        strk.cnt += 16
        inst.then_inc(strk.sem, 16)
        key = "d:%d" % id(strk)
        rec = (key, strk.sem, strk.cnt)
        for t in writes:
            t.w = rec
            t.r = {}
        for t in reads:
            t.r[key] = (strk.sem, strk.cnt)
        return inst

    def barrier(self):
        for en, e in self.eng.items():
            for on, o in self.eng.items():
                if on == en or on == "sp" or o["cnt"] == 0:
                    continue
                if e["waited"].get(on, 0) < o["cnt"]:
                    e["h"].wait_ge(o["sem"], o["cnt"])
                    e["waited"][on] = o["cnt"]
            for t in self.dma_trks:
                k = "d:%d" % id(t)
                if t.cnt and e["waited"].get(k, 0) < t.cnt:
                    e["h"].wait_ge(t.sem, t.cnt)
                    e["waited"][k] = t.cnt

    def finish(self):
        e = self.eng["sp"]
        for t in self.dma_trks:
            e["h"].wait_ge(t.sem, t.cnt)


def run_interleaved(gens, width):
    it = iter(gens)
    active = []
    more = True
    while True:
        while more and len(active) < width:
            try:
                active.append(next(it))
            except StopIteration:
                more = False
        if not active:
            break
        for g in list(active):
            try:
                next(g)
            except StopIteration:
                active.remove(g)


def interleave_gen(gens, width):
    it = iter(gens)
    active = []
    more = True
    while True:
        while more and len(active) < width:
            try:
                active.append(next(it))
            except StopIteration:
                more = False
        if not active:
            return
        for g in list(active):
            try:
                next(g)
            except StopIteration:
                active.remove(g)
        yield


def multi(g, k):
    while True:
        for _ in range(k):
            try:
                next(g)
            except StopIteration:
                return
        yield


def build_program(jobs):
    NJ = len(jobs)
    SQ = jobs[0][0]
    assert all(j[0] == SQ for j in jobs)
    NB = SQ // 512
    SOT = sum(j[1] for j in jobs)
    SOA = max(SOT, 128)
    NKT_MAX = max((j[0] + j[1]) // 128 for j in jobs)

    nc = bass.Bass("TRN2", target_bir_lowering=False)

    def din(name, shape):
        return nc.dram_tensor(name, shape, F32, kind="ExternalInput").ap()

    xq = din("xq", [NJ, SQ, D])
    xo = din("xo", [SOA, D])
    xh = din("xh", [NJ, NB, 16, D])
    ropeq = din("ropeq", [NJ, SQ, 128])
    ropeo = din("ropeo", [SOA, 128])
    hmask_d = din("hmask", [128, NJ * NB * 16])
    icnt_d = din("icnt", [128, NJ * 2 * 32])
    cT_d = din("cT", [128, 8 * NJ])
    w_ada_d = din("w_ada", [128, 8, 6144])
    b_adaT_d = din("b_adaT", [128, 48])
    gvec_d = din("gvec", [128, 32])
    gqk_d = din("gqk", [128, 640])
    w_in_d = din("w_in", [128, 8, DIN])
    w_out_d = din("w_out", [128, 8, D])
    w_pool_d = din("w_pool", [128, 4, 128])
    pscale_d = din("pscale", [128, 4])
    w_ff1_d = din("w_ff1", [128, 8, DFF])
    w_ff2_d = din("w_ff2", [128, 32, D])
    yout = nc.dram_tensor("y", [NJ, SQ, D], F32, kind="ExternalOutput").ap()
    x1buf = nc.dram_tensor("x1buf", [NJ, SQ, D], F32).ap()

    S = Sched(nc)
    op = S.op

    ps = nc.alloc_psum_tensor("ps", [128, 7 * 512], F32)
    psb7 = nc.alloc_psum_tensor("psb7", [128, 1024], BF16)
    psbs = [psb7[:, :]] + [ps[:, i * 512:(i + 1) * 512].bitcast(BF16) for i in (6, 5, 4)]
    PB = [Trk("bank%d" % i) for i in range(8)]
    PBB = [PB[7], PB[6], PB[5], PB[4]]

    def bank(i, n=512, parts=slice(0, 128)):
        return ps[parts, i * 512:i * 512 + n]

    with ExitStack() as glob:
        def sb(name, shape, dt=F32, stack=glob):
            return stack.enter_context(nc.sbuf_tensor("sb_" + name, shape, dt))

        ident = sb("ident", [128, 128], BF16); Tident = Trk()
        identf = sb("identf", [128, 128], F32); Tidentf = Trk()
        eps_t = sb("eps_t", [128, 1]); Teps = Trk()
        modD = sb("modD", [128, 6, 8, NJ]); TmodD = Trk()
        gqk = sb("gqk", [128, 640]); Tgqk = Trk()
        hmask = sb("hmask", [128, NJ * NB * 16]); Thmask = Trk()
        icnt = sb("icnt", [128, NJ * 2 * 32]); Ticnt = Trk()
        pscale = sb("pscale", [128, 4]); Tpscale = Trk()
        grow = sb("grow", [128, D]); Tgrow = Trk()
        rep = sb("rep", [128, 8, 128]); Trep = Trk()
        NSS = 8
        ss_t = [sb("ss%d" % i, [128, 2]) for i in range(NSS)]; Tss = [Trk() for _ in range(NSS)]
        ss_i = [0]

        for t_, tr_ in ((ident, Tident), (identf, Tidentf)):
            op("pool", lambda h, t_=t_: h.memset(t_[:], 0.0), writes=[tr_])
            op("pool", lambda h, t_=t_: h.affine_select(out=t_[:], in_=t_[:], pattern=[[-1, 128]],
                                                       compare_op=ALU.not_equal, fill=1.0, base=0,
                                                       channel_multiplier=1), writes=[tr_])
        op("pool", lambda h: h.memset(eps_t[:], EPS), writes=[Teps])
        S.dma(gqk[:], gqk_d[:, :], Tgqk, writes=[Tgqk])
        S.dma(hmask[:], hmask_d[:, :], Thmask, writes=[Thmask])
        S.dma(icnt[:], icnt_d[:, :], Ticnt, writes=[Ticnt])
        S.dma(pscale[:], pscale_d[:, :], Tpscale, writes=[Tpscale])

        with ExitStack() as st:
            wa = [sb("wa%d" % i, [128, 8, 1024], F32, st) for i in range(2)]; Twa = [Trk(), Trk()]
            cT = sb("cT", [128, 8 * NJ], F32, st); TcT = Trk()
            e_t = sb("e_t", [128, 8 * NJ], F32, st); Te = Trk()
            sT = sb("sT", [128, 8 * NJ], F32, st); TsT = Trk()
            b_adaT = sb("b_adaT", [128, 48], F32, st); Tb = Trk()
            gvec = sb("gvec", [128, 32], F32, st); Tg = Trk()
            modT = sb("modT", [128, 6, 8, NJ], F32, st); TmodT = Trk()
            S.dma(cT[:], cT_d[:, :], TcT, writes=[TcT])
            S.dma(b_adaT[:], b_adaT_d[:, :], Tb, writes=[Tb])
            S.dma(gvec[:], gvec_d[:, :], Tg, writes=[Tg])
            op("act", lambda h: h.activation(out=e_t[:], in_=cT[:], func=AF.Exp, scale=-1.0),
               reads=[TcT], writes=[Te])
            op("dve", lambda h: h.tensor_scalar_add(out=e_t[:], in0=e_t[:], scalar1=1.0), writes=[Te])
            op("dve", lambda h: h.reciprocal(out=e_t[:], in_=e_t[:]), writes=[Te])
            op("dve", lambda h: h.tensor_mul(out=sT[:], in0=cT[:], in1=e_t[:]), reads=[TcT, Te], writes=[TsT])
            sT3 = sT[:, :].rearrange("p (k j) -> p k j", j=NJ)
            for blk in range(6):
                b = blk % 2
                S.dma(wa[b][:], w_ada_d[:, :, blk * 1024:(blk + 1) * 1024], Twa[b], writes=[Twa[b]])
                for mc in range(8):
                    for k in range(8):
                        op("pe", lambda h, b=b, mc=mc, k=k: h.matmul(
                            bank(b)[:, mc * NJ:(mc + 1) * NJ], lhsT=wa[b][:, k, mc * 128:(mc + 1) * 128],
                            rhs=sT3[:, k, :], start=(k == 0), stop=(k == 7)),
                           reads=[Twa[b], TsT], writes=[PB[b]])
                op("dve", lambda h, b=b, blk=blk: h.tensor_tensor(
                    out=modT[:, blk, :, :], in0=bank(b)[:, 0:8 * NJ].rearrange("p (m j) -> p m j", j=NJ),
                    in1=b_adaT[:, blk * 8:(blk + 1) * 8].unsqueeze(2).broadcast_to([128, 8, NJ]), op=ALU.add),
                   reads=[Tb], writes=[TmodT, PB[b]])

            def gb(i):
                return gvec[:, i * 8:(i + 1) * 8].unsqueeze(2).broadcast_to([128, 8, NJ])
            op("dve", lambda h: h.scalar_tensor_tensor(out=modD[:, 0], in0=modT[:, 1], scalar=1.0, in1=gb(0),
                                                       op0=ALU.add, op1=ALU.mult), reads=[TmodT, Tg], writes=[TmodD])
            op("dve", lambda h: h.tensor_copy(out=modD[:, 1], in_=modT[:, 0]), reads=[TmodT], writes=[TmodD])
            op("dve", lambda h: h.tensor_tensor(out=modD[:, 2], in0=modT[:, 2], in1=gb(1), op=ALU.mult),
               reads=[TmodT, Tg], writes=[TmodD])
            op("dve", lambda h: h.scalar_tensor_tensor(out=modD[:, 3], in0=modT[:, 4], scalar=1.0, in1=gb(2),
                                                       op0=ALU.add, op1=ALU.mult), reads=[TmodT, Tg], writes=[TmodD])
            op("dve", lambda h: h.tensor_copy(out=modD[:, 4], in_=modT[:, 3]), reads=[TmodT], writes=[TmodD])
            op("dve", lambda h: h.tensor_tensor(out=modD[:, 5], in0=modT[:, 5], in1=gb(3), op=ALU.mult),
               reads=[TmodT, Tg], writes=[TmodD])
            S.barrier()

        def build_grow(j, which):
            op("dve", lambda h: h.tensor_copy(out=rep[:], in_=modD[:, which, :, j].unsqueeze(2).broadcast_to([128, 8, 128])),
               reads=[TmodD], writes=[Trep])
            for k in range(8):
                bk = 4 + k // 4
                op("pe", lambda h, k=k, bk=bk: h.transpose(out=bank(bk)[:, (k % 4) * 128:(k % 4 + 1) * 128],
                                                          in_=rep[:, k, :], identity=identf[:]),
                   reads=[Trep, Tidentf], writes=[PB[bk]])
            for half in range(2):
                op("act", lambda h, half=half: h.activation(out=grow[:, half * 512:(half + 1) * 512],
                                                            in_=bank(4 + half)[:, :], func=AF.Copy),
                   writes=[Tgrow, PB[4 + half]])

        def rstd_from_ss(ssv, Tssv, rows, inv_n):
            op("act", lambda h: h.activation(out=ssv, in_=ssv, func=AF.Ln, scale=inv_n, bias=eps_t[0:rows, 0:1]),
               reads=[Teps], writes=[Tssv])
            op("act", lambda h: h.activation(out=ssv, in_=ssv, func=AF.Exp, scale=-0.5), writes=[Tssv])

        def next_ss():
            i = ss_i[0] % NSS
            ss_i[0] += 1
            return ss_t[i], Tss[i]

        def norm_a(xt_, Txt, rows, xn, Txn, ss_ext=None):
            sst, Ts = ss_ext if ss_ext is not None else next_ss()
            op("pool", lambda h: h.memset(sst[0:rows, 0:1], 0.0), writes=[Ts])
            yield
            op("act", lambda h: h.activation(out=xn[0:rows, :], in_=xt_[0:rows, :], func=AF.Square,
                                             accum_out=sst[0:rows, 0:1]),
               reads=[Txt], writes=[Ts, Txn])
            yield
            rstd_from_ss(sst[0:rows, 0:1], Ts, rows, 1.0 / D)
            yield
            op("dve", lambda h: h.tensor_scalar_mul(out=xn[0:rows, :], in0=xt_[0:rows, :], scalar1=sst[0:rows, 0:1]),
               reads=[Txt, Ts], writes=[Txn])
            yield

        def norm_b(rows, j, gsi, shi, hT_dst, Thd, xn, Txn, pi, act_split=False):
            psb = psbs[pi]
            for k in range(8):
                op("pe", lambda h, k=k: h.transpose(out=psb[:, k * rows:(k + 1) * rows],
                                                    in_=xn[0:rows, k * 128:(k + 1) * 128],
                                                    identity=ident[0:rows, 0:rows]),
                   reads=[Txn, Tident], writes=[PBB[pi]])
            yield
            for k in range(8):
                if act_split and k % 2 == 1:
                    op("act", lambda h, k=k: h.activation(out=hT_dst(k), in_=psb[:, k * rows:(k + 1) * rows], func=AF.Identity,
                                                          scale=modD[:, gsi, k, j:j + 1], bias=modD[:, shi, k, j:j + 1]),
                       reads=[TmodD], writes=[Thd, PBB[pi]])
                else:
                    op("dve", lambda h, k=k: h.tensor_scalar(out=hT_dst(k), in0=psb[:, k * rows:(k + 1) * rows],
                                                             scalar1=modD[:, gsi, k, j:j + 1], scalar2=modD[:, shi, k, j:j + 1],
                                                             op0=ALU.mult, op1=ALU.add),
                       reads=[TmodD], writes=[Thd, PBB[pi]])
                if k % 2 == 1:
                    yield

        def resid_epilogue(b0, xt_, Txt, tmp, Ttmp, dst_ap, Tdst, add_x=True):
            sst, Ts = next_ss()
            mv = ps[:, b0 * 512:b0 * 512 + 1024]
            op("pool", lambda h: h.memset(sst[:, 0:1], 0.0), writes=[Ts])
            op("act", lambda h: h.activation(out=tmp, in_=mv, func=AF.Square, accum_out=sst[:, 0:1]),
               writes=[Ts, Ttmp, PB[b0], PB[b0 + 1]])
            rstd_from_ss(sst[:, 0:1], Ts, 128, 1.0 / D)
            op("dve", lambda h: h.scalar_tensor_tensor(out=tmp, in0=mv, scalar=sst[:, 0:1], in1=grow[:],
                                                       op0=ALU.mult, op1=ALU.mult),
               reads=[Ts, Tgrow], writes=[Ttmp, PB[b0], PB[b0 + 1]])
            if add_x:
                op("pool", lambda h: h.tensor_tensor(out=xt_[:], in0=xt_[:], in1=tmp, op=ALU.add),
                   reads=[Ttmp], writes=[Txt])
                S.dma(dst_ap, xt_[:], Txt, reads=[Txt], writes=[Tdst])
            else:
                S.dma(dst_ap, tmp, Ttmp, reads=[Ttmp], writes=[Tdst])

        def load_cast(dst_view_fn, src_view_fn, nchunks, stg, Tstg, Tdst, width):
            for i in range(nchunks):
                b = i % 2
                S.dma(stg[b][:, 0:width], src_view_fn(i), Tstg[b], writes=[Tstg[b]])
                eng = "pool" if i % 2 == 0 else "dve"
                op(eng, lambda h, i=i, b=b: h.tensor_copy(out=dst_view_fn(i), in_=stg[b][:, 0:width]),
                   reads=[Tstg[b]], writes=[Tdst])

        X1T = [[Trk() for _ in range(SQ // 128)] for _ in range(NJ)]
        with ExitStack() as st:
            w_in = sb("w_in_bf", [128, 8, DIN], BF16, st); Twin = Trk()
            w_out = sb("w_out_bf", [128, 8, D], BF16, st); Twout = Trk()
            w_pool = sb("w_pool_bf", [128, 4, 128], BF16, st); Twpool = Trk()
            kT = sb("kT", [128, NKT_MAX * 128], BF16, st); TkT = Trk()
            rstd_all = sb("rstd_all", [128, SQ // 128, 2], F32, st); Trstd = [Trk() for _ in range(SQ // 128)]
            vaug = sb("vaug", [128, NKT_MAX, 2, 128], BF16, st); Tvaug = Trk()
            with ExitStack() as st2:
                stg = [sb("stg%d" % i, [128, 2560], F32, st2) for i in range(2)]; Tstg = [Trk(), Trk()]
                load_cast(lambda i: w_in[:, 2 * i:2 * i + 2, :].rearrange("p a n -> p (a n)"),
                          lambda i: w_in_d[:, 2 * i:2 * i + 2, :].rearrange("p a n -> p (a n)"), 4, stg, Tstg, Twin, 2560)
                load_cast(lambda i: w_out[:, 2 * i:2 * i + 2, :].rearrange("p a n -> p (a n)"),
                          lambda i: w_out_d[:, 2 * i:2 * i + 2, :].rearrange("p a n -> p (a n)"), 4, stg, Tstg, Twout, 2048)
                load_cast(lambda i: w_pool[:, :, :].rearrange("p a n -> p (a n)"),
                          lambda i: w_pool_d[:, :, :].rearrange("p a n -> p (a n)"), 1, stg, Tstg, Twpool, 512)
                S.barrier()
            xts = [sb("xt%d" % i, [128, D], F32, st) for i in range(2)]; Txts = [Trk(), Trk()]
            xn = [sb("xn%d" % i, [128, D], BF16, st) for i in range(2)]; Txn = [Trk(), Trk()]
            hTt = [sb("hTt%d" % i, [128, 8, 128], BF16, st) for i in range(2)]; ThTt = [Trk(), Trk()]
            hTb = sb("hTb", [128, 8, 512], BF16, st); ThTb = [Trk() for _ in range(4)]
            hTh = sb("hTh", [128, 8, 16], BF16, st); ThTh = Trk()
            ropet = sb("ropet", [128, 4, 128], F32, st); Tropet = Trk()
            ropek = [sb("ropek%d" % i, [128, 128], F32, st) for i in range(2)]; Tropek = [Trk(), Trk()]
            qtmp0 = sb("qtmp0", [128, 4, 512], F32, st); Tqtmp0 = [Trk() for _ in range(4)]
            Wt = sb("Wt", [128, 4, 512], F32, st); TWt = [Trk() for _ in range(4)]
            ktmp = [sb("ktmp%d" % i, [128, 4, 128], F32, st) for i in range(2)]; Tktmp = [[Trk() for _ in range(4)] for _ in range(2)]
            qss = [sb("qss%d" % i, [128, 8], F32, st) for i in range(2)]; Tqss = [Trk(), Trk()]
            qr = [sb("qr%d" % i, [128, 512], BF16, st) for i in range(2)]; Tqr = [Trk(), Trk()]
            qTbs = [sb("qTb%d" % i, [128, 4, 512], BF16, st) for i in range(2)]; TqTbs = [Trk(), Trk()]
            uT = sb("uT", [128, 4, 528], F32, st); TuT = Trk()
            sA = sb("sA", [128, 528], F32, st); TsA = Trk()
            sB = sb("sB", [128, 528], F32, st); TsB = Trk()
            sC = sb("sC", [128, 528], F32, st); TsC = Trk()
            sD_ = sb("sD", [128, 528], F32, st); TsD = Trk()
            tmp8 = sb("tmp8", [128, 4, 8], F32, st); Ttmp8 = Trk()
            mixT = sb("mixT", [128, 4, 512], BF16, st); TmixT = Trk()
            aTs = [sb("aT%d" % i, [128, 4, 512], BF16, st) for i in range(2)]; TaTs = [Trk(), Trk()]
            aT = aTs[0]
            pTs = [sb("pT%d" % i, [128, 4, 512], BF16, st) for i in range(2)]; TpTs = [Trk(), Trk()]
            pt = [sb("pt%d" % i, [128, 1024], BF16, st) for i in range(3)]; Tpt = [Trk() for _ in range(3)]
            recip = sb("recip", [128, 512], F32, st); Trecip = Trk()
            osb = [sb("osb%d" % i, [128, 512], F32, st) for i in range(2)]; Tosb = [Trk(), Trk()]
            tmpo = rep[:, :, :].rearrange("p a n -> p (a n)"); Ttmpo = Trep
            op("pool", lambda h: h.memset(vaug[:], 1.0), writes=[Tvaug])
            kr_x = [sb("krx%d" % i, [128, 128], BF16, st) for i in range(2)]
            qss_x = [sb("qssx%d" % i, [128, 8], F32, st) for i in range(2)]
            K_xts = xts + [Wt[:, 0:2, :].rearrange("p a n -> p (a n)"), Wt[:, 2:4, :].rearrange("p a n -> p (a n)")]
            K_Txts = Txts + [Trk(), Trk()]
            K_xn = xn + [aT[:, 0:2, :].rearrange("p a n -> p (a n)"), aT[:, 2:4, :].rearrange("p a n -> p (a n)")]
            K_Txn = Txn + [Trk(), Trk()]
            K_hTt = hTt + [mixT[:, 0:2, :].rearrange("p a (k c) -> p (a k) c", c=128),
                           mixT[:, 2:4, :].rearrange("p a (k c) -> p (a k) c", c=128)]
            K_ThTt = ThTt + [Trk(), Trk()]
            K_ktmp = ktmp + [sA[:, 0:512].rearrange("p (i c) -> p i c", c=128), sB[:, 0:512].rearrange("p (i c) -> p i c", c=128)]
            K_Tktmp = Tktmp + [[Trk() for _ in range(4)] for _ in range(2)]
            K_ropek = ropek + [sC[:, 0:128], sD_[:, 0:128]]
            K_Tropek = Tropek + [Trk(), Trk()]
            K_qss = qss + qss_x
            K_Tqss = Tqss + [Trk(), Trk()]
            K_qr = [qr[0][:, 0:128], qr[1][:, 0:128], kr_x[0][:, :], kr_x[1][:, :]]
            K_Tqr = Tqr + [Trk(), Trk()]

            def qk_process(psv, Tbank, nh, gsl, ropeC, ropeS, Trope, out_bf, Tout, tmps, Ttmps, qs, Tqs):
                n = nh * 64
                v3 = lambda a: a.rearrange("p (h d) -> p h d", d=64)
                qsq, qxn, qt1, qt2 = tmps
                Tqsq, Tqxn, Tqt1, Tqt2 = Ttmps
                op("act", lambda h: h.activation(out=qsq, in_=psv, func=AF.Square), writes=[Tqsq, Tbank])
                yield
                op("dve", lambda h: h.tensor_reduce(out=qs[:, 0:nh], in_=v3(qsq), axis=AX.X, op=ALU.add),
                   reads=[Tqsq], writes=[Tqs])
                yield
                rstd_from_ss(qs[:, 0:nh], Tqs, 128, 1.0 / 64)
                yield
                op("dve", lambda h: h.tensor_tensor(out=v3(qxn), in0=v3(psv),
                                                    in1=qs[:, 0:nh].unsqueeze(2).broadcast_to([128, nh, 64]), op=ALU.mult),
                   reads=[Tqs], writes=[Tqxn, Tbank])
                yield
                op("pool", lambda h: h.tensor_tensor(out=qxn, in0=qxn, in1=gsl, op=ALU.mult),
                   reads=[Tgqk], writes=[Tqxn])
                yield
                x3 = v3(qxn)
                op("dve", lambda h: h.tensor_tensor(out=v3(qt1), in0=x3,
                                                    in1=ropeC.unsqueeze(1).broadcast_to([128, nh, 64]), op=ALU.mult),
                   reads=[Tqxn, Trope], writes=[Tqt1])
                t23 = v3(qt2)
                op("pool", lambda h: h.tensor_tensor(out=t23[:, :, 0:32], in0=x3[:, :, 32:64],
                                                     in1=ropeS[:, 0:32].unsqueeze(1).broadcast_to([128, nh, 32]), op=ALU.mult),
                   reads=[Tqxn, Trope], writes=[Tqt2])
                yield
                op("pool", lambda h: h.tensor_tensor(out=t23[:, :, 32:64], in0=x3[:, :, 0:32],
                                                     in1=ropeS[:, 32:64].unsqueeze(1).broadcast_to([128, nh, 32]), op=ALU.mult),
                   reads=[Tqxn, Trope], writes=[Tqt2])
                yield
                op("dve", lambda h: h.tensor_tensor(out=out_bf, in0=qt1, in1=qt2, op=ALU.add),
                   reads=[Tqt1, Tqt2], writes=[Tout])
                yield

            def kv_tile(j, kt, src, rsrc, si):
                S.dma(K_xts[si][:], src, K_Txts[si], writes=[K_Txts[si]])
                S.dma(K_ropek[si][:], rsrc, K_Tropek[si], writes=[K_Tropek[si]])
                yield
                yield from norm_a(K_xts[si], K_Txts[si], 128, K_xn[si], K_Txn[si],
                                  (rstd_all[:, kt, :], Trstd[kt]) if kt < SQ // 128 else None)
                yield from norm_b(128, j, 0, 1, lambda k: K_hTt[si][:, k, :], K_ThTt[si], K_xn[si], K_Txn[si], si, act_split=True)
                for k in range(8):
                    op("pe", lambda h, k=k: h.matmul(bank(si)[:, 0:256], lhsT=K_hTt[si][:, k, :], rhs=w_in[:, k, 512:768],
                                                     start=(k == 0), stop=(k == 7)),
                       reads=[K_ThTt[si], Twin], writes=[PB[si]])
                yield
                op("act", lambda h: h.activation(out=vaug[:, kt, 0, 0:64], in_=bank(si)[:, 128:192], func=AF.Copy),
                   writes=[Tvaug, PB[si]])
                op("act", lambda h: h.activation(out=vaug[:, kt, 1, 64:128], in_=bank(si)[:, 192:256], func=AF.Copy),
                   writes=[Tvaug, PB[si]])
                yield
                yield from qk_process(bank(si)[:, 0:128], PB[si], 2, gqk[:, 512:640], K_ropek[si][:, 0:64], K_ropek[si][:, 64:128],
                                      K_Tropek[si], K_qr[si], K_Tqr[si],
                                      [K_ktmp[si][:, i, :] for i in range(4)], K_Tktmp[si], K_qss[si], K_Tqss[si])
                op("pe", lambda h: h.transpose(out=psbs[si][:, 0:128], in_=K_qr[si], identity=ident[:]),
                   reads=[K_Tqr[si], Tident], writes=[PBB[si]])
                yield
                op("dve", lambda h: h.tensor_copy(out=kT[:, kt * 128:(kt + 1) * 128], in_=psbs[si][:, 0:128]),
                   writes=[TkT, PBB[si]])
                yield

            def q_tile(j, r0, t, si, qset):
                qTb = qTbs[qset]; TqTb = TqTbs[qset]
                pi = 1 - si
                fb = bank(6)[:, :] if si == 0 else psb7[:, :].bitcast(F32)
                Tfb = PB[6] if si == 0 else PB[7]
                ti_ = (r0 + t * 128) // 128
                op("dve", lambda h: h.tensor_scalar_mul(out=xn[si][:, :], in0=xts[si][:, :], scalar1=rstd_all[:, ti_, 0:1]),
                   reads=[Txts[si], Trstd[ti_]], writes=[Txn[si]])
                yield
                if t + 2 < 4:
                    S.dma(xts[si][:], xq[j, r0 + (t + 2) * 128:r0 + (t + 3) * 128, :], Txts[si], writes=[Txts[si]])
                yield
                yield from norm_b(128, j, 0, 1, lambda k: hTb[:, k, t * 128:(t + 1) * 128], ThTb[t], xn[si], Txn[si], pi)
                yield
                for k in range(8):
                    op("pe", lambda h, k=k: h.matmul(fb, lhsT=hTb[:, k, t * 128:(t + 1) * 128],
                                                     rhs=w_in[:, k, 0:512], start=(k == 0), stop=(k == 7)),
                       reads=[ThTb[t], Twin], writes=[Tfb])
                yield
                if si == 0:
                    tm = [qtmp0[:, i, :] for i in range(4)]; Ttm = Tqtmp0
                else:
                    tm = [Wt[:, i, :] for i in range(4)]; Ttm = TWt
                yield from qk_process(fb, Tfb, 8, gqk[:, 0:512], ropet[:, t, 0:64], ropet[:, t, 64:128], Tropet,
                                      qr[si][:, :], Tqr[si], tm, Ttm, qss[si], Tqss[si])
                yield
                for s in range(4):
                    op("pe", lambda h, s=s: h.transpose(out=psbs[pi][:, s * 128:(s + 1) * 128], in_=qr[si][:, s * 128:(s + 1) * 128],
                                                        identity=ident[:]),
                       reads=[Tqr[si], Tident], writes=[PBB[pi]])
                yield
                op("dve", lambda h: h.tensor_copy(out=qTb[:, :, t * 128:(t + 1) * 128],
                                                  in_=psbs[pi][:, 0:512].rearrange("p (s c) -> p s c", c=128)),
                   writes=[TqTb, PBB[pi]])
                yield

            def halo_chain(j, b):
                S.dma(xts[0][0:16, :], xh[j, b, :, :], Txts[0], writes=[Txts[0]])
                yield
                yield from norm_a(xts[0], Txts[0], 16, xn[0], Txn[0])
                yield from norm_b(16, j, 0, 1, lambda k: hTh[:, k, :], ThTh, xn[0], Txn[0], 1)

            ooff = 0
            for j, (SQj, SOj) in enumerate(jobs):
                NTO = SQj // 128
                NKT = (SQj + SOj) // 128
                build_grow(j, 2)
                def kv_gens():
                    for kt in range(NKT):
                        if kt < NTO:
                            src = xq[j, kt * 128:(kt + 1) * 128, :]
                            rsrc = ropeq[j, kt * 128:(kt + 1) * 128, :]
                        else:
                            o = ooff + (kt - NTO) * 128
                            src = xo[o:o + 128, :]
                            rsrc = ropeo[o:o + 128, :]
                        yield kv_tile(j, kt, src, rsrc, kt % 4)
                S.barrier()
                run_interleaved(kv_gens(), 4)
                S.barrier()

                NBLK = SQj // 512

                def pphase(j, b, qset, wait_flag=None, part="all"):
                    r0 = b * 512
                    pT = pTs[qset]; TpT = TpTs[qset]
                    if part in ("all", "q"):
                        S.dma(ropet[:], ropeq[j, r0:r0 + 512, :].rearrange("(t p) c -> p t c", p=128), Tropet, writes=[Tropet])
                        for t0 in range(2):
                            S.dma(xts[t0][:], xq[j, r0 + t0 * 128:r0 + (t0 + 1) * 128, :], Txts[t0], writes=[Txts[t0]])
                        yield
                        yield
                        yield from interleave_gen((q_tile(j, r0, t, t % 2, qset) for t in range(4)), 2)
                    if part == "q":
                        return
                    yield from halo_chain(j, b)
                    for g in range(4):
                        for k in range(8):
                            op("pe", lambda h, k=k, g=g: h.matmul(bank(6)[:, :], lhsT=w_in[:, k, 768 + g * 128:768 + (g + 1) * 128],
                                                                  rhs=hTb[:, k, :], start=(k == 0), stop=(k == 7)),
                               reads=ThTb + [Twin], writes=[PB[6]])
                        yield
                        yield
                        op("dve", lambda h, g=g: h.tensor_copy(out=uT[:, g, 8:520], in_=bank(6)[:, :]),
                           writes=[TuT, PB[6]])
                        yield
                    for g in range(4):
                        for k in range(8):
                            op("pe", lambda h, k=k, g=g: h.matmul(bank(6)[:, g * 16:(g + 1) * 16],
                                                                  lhsT=w_in[:, k, 768 + g * 128:768 + (g + 1) * 128],
                                                                  rhs=hTh[:, k, :], start=(k == 0), stop=(k == 7)),
                               reads=[ThTh, Twin], writes=[PB[6]])
                    yield
                    hv = bank(6)[:, 0:64].rearrange("p (g e) -> p g e", e=16)
                    mo = (j * NB + b) * 16
                    op("dve", lambda h: h.tensor_tensor(out=uT[:, :, 0:8], in0=hv[:, :, 0:8],
                                                        in1=hmask[:, mo:mo + 8].unsqueeze(1).broadcast_to([128, 4, 8]), op=ALU.mult),
                       reads=[Thmask], writes=[TuT, PB[6]])
                    op("dve", lambda h: h.tensor_tensor(out=uT[:, :, 520:528], in0=hv[:, :, 8:16],
                                                        in1=hmask[:, mo + 8:mo + 16].unsqueeze(1).broadcast_to([128, 4, 8]), op=ALU.mult),
                       reads=[Thmask], writes=[TuT, PB[6]])
                    yield
                    tt = lambda h, o, a, b_: h.tensor_tensor(out=o, in0=a, in1=b_, op=ALU.add)
                    E = lambda g: uT[:, g, :]
                    op("pool", lambda h: tt(h, sC[:, 0:527], E(3)[:, 0:527], E(3)[:, 1:528]), reads=[TuT], writes=[TsC])
                    op("dve", lambda h: tt(h, Wt[:, 0, :], E(0)[:, 7:519], E(0)[:, 8:520]), reads=[TuT], writes=[TWt[0]])
                    yield
                    op("dve", lambda h: tt(h, sA[:, 0:527], E(1)[:, 0:527], E(1)[:, 1:528]), reads=[TuT], writes=[TsA])
                    op("pool", lambda h: tt(h, sD_[:, 0:525], sC[:, 0:525], sC[:, 2:527]), reads=[TsC], writes=[TsD])
                    yield
                    op("dve", lambda h: tt(h, Wt[:, 1, :], sA[:, 6:518], sA[:, 8:520]), reads=[TsA], writes=[TWt[1]])
                    yield
                    op("dve", lambda h: tt(h, sB[:, 0:527], E(2)[:, 0:527], E(2)[:, 1:528]), reads=[TuT], writes=[TsB])
                    op("pool", lambda h: tt(h, sC[:, 0:521], sD_[:, 0:521], sD_[:, 4:525]), reads=[TsD], writes=[TsC])
                    yield
                    op("dve", lambda h: tt(h, sA[:, 0:525], sB[:, 0:525], sB[:, 2:527]), reads=[TsB], writes=[TsA])
                    yield
                    op("dve", lambda h: tt(h, Wt[:, 2, :], sA[:, 4:516], sA[:, 8:520]), reads=[TsA], writes=[TWt[2]])
                    op("pool", lambda h: tt(h, Wt[:, 3, :], sC[:, 0:512], sC[:, 8:520]), reads=[TsC], writes=[TWt[3]])
                    yield
                    for g in range(4):
                        op("dve", lambda h, g=g: h.scalar_tensor_tensor(out=mixT[:, g, :], in0=Wt[:, g, :], scalar=1.0 / POOL_W[g],
                                                                        in1=uT[:, g, 8:520], op0=ALU.mult, op1=ALU.subtract),
                           reads=[TWt[g], TuT], writes=[TmixT])
                        yield
                    for edge, do in ((0, b == 0), (1, b == NBLK - 1)):
                        if not do:
                            continue
                        io = (j * 2 + edge) * 32
                        c0 = 0 if edge == 0 else 504
                        op("dve", lambda h, io=io, c0=c0: h.tensor_tensor(
                            out=tmp8[:], in0=Wt[:, :, c0:c0 + 8],
                            in1=icnt[:, io:io + 32].rearrange("p (g e) -> p g e", e=8), op=ALU.mult),
                           reads=TWt + [Ticnt], writes=[Ttmp8])
                        op("dve", lambda h, c0=c0: h.tensor_tensor(out=mixT[:, :, c0:c0 + 8], in0=tmp8[:],
                                                                   in1=uT[:, :, 8 + c0:16 + c0], op=ALU.subtract),
                           reads=[Ttmp8, TuT], writes=[TmixT])
                        yield
                    while wait_flag is not None and not wait_flag[0]:
                        yield
                    for g in range(4):
                        op("pe", lambda h, g=g: h.matmul(bank(6)[:, :], lhsT=w_pool[:, g, :], rhs=mixT[:, g, :],
                                                         start=True, stop=True),
                           reads=[TmixT, Twpool], writes=[PB[6]])
                        yield
                        yield
                        op("dve", lambda h, g=g: h.tensor_scalar_mul(out=pT[:, g, :], in0=bank(6)[:, :], scalar1=pscale[:, g:g + 1]),
                           reads=[Tpscale], writes=[TpT, PB[6]])
                        yield

                def attn(j, b, qset):
                    qTb = qTbs[qset]; TqTb = TqTbs[qset]
                    aTb = aTs[b % 2]; TaT = TaTs[b % 2]
                    lo = slice(0, 64)
                    hi = slice(64, 128)
                    groups = [(s, jc) for s in range(4) for jc in range(NKT)]
                    NGR = len(groups)

                    def qk_group(i):
                        s, jc = groups[i]
                        sb_i = i % 2
                        op("pe", lambda h: h.matmul(bank(sb_i * 2)[:, :], lhsT=kT[lo, jc * 128:(jc + 1) * 128],
                                                    rhs=qTb[lo, s, :], start=True, stop=True),
                           reads=[TkT, TqTb], writes=[PB[sb_i * 2]])
                        op("pe", lambda h: h.matmul(bank(sb_i * 2 + 1)[:, :], lhsT=kT[hi, jc * 128:(jc + 1) * 128],
                                                    rhs=qTb[hi, s, :], start=True, stop=True),
                           reads=[TkT, TqTb], writes=[PB[sb_i * 2 + 1]])

                    qk_group(0)
                    qk_group(1)

                    def queue_norm(s):
                        for q4 in range(4):
                            cs = slice(q4 * 128, (q4 + 1) * 128)
                            pending.append(lambda cs=cs: op("dve", lambda h: h.reciprocal(out=recip[lo, cs], in_=osb[0][hi, cs]),
                                                            reads=[Tosb[0]], writes=[Trecip]))
                            pending.append(lambda cs=cs, s=s: op("pool", lambda h: h.tensor_tensor(
                                out=aTb[lo, s, cs], in0=osb[0][lo, cs], in1=recip[lo, cs], op=ALU.mult),
                                reads=[Trecip, Tosb[0]], writes=[TaT]))
                            pending.append(lambda cs=cs: op("dve", lambda h: h.reciprocal(out=recip[hi, cs], in_=osb[1][lo, cs]),
                                                            reads=[Tosb[1]], writes=[Trecip]))
                            pending.append(lambda cs=cs, s=s: op("pool", lambda h: h.tensor_tensor(
                                out=aTb[hi, s, cs], in0=osb[1][hi, cs], in1=recip[hi, cs], op=ALU.mult),
                                reads=[Trecip, Tosb[1]], writes=[TaT]))

                    for i, (s, jc) in enumerate(groups):
                        sb_i = i % 2
                        pi = i % 3
                        op("act", lambda h: h.activation(out=pt[pi][:, :], in_=ps[:, sb_i * 1024:(sb_i + 1) * 1024],
                                                         func=AF.Exp, scale=0.125),
                           writes=[Tpt[pi], PB[sb_i * 2], PB[sb_i * 2 + 1]])
                        if i + 2 < NGR:
                            qk_group(i + 2)
                        op("pe", lambda h: h.matmul(bank(4)[:, :], lhsT=vaug[:, jc, 0, :], rhs=pt[pi][:, 0:512],
                                                    start=(jc == 0), stop=(jc == NKT - 1)),
                           reads=[Tvaug, Tpt[pi]], writes=[PB[4]])
                        op("pe", lambda h: h.matmul(bank(5)[:, :], lhsT=vaug[:, jc, 1, :], rhs=pt[pi][:, 512:1024],
                                                    start=(jc == 0), stop=(jc == NKT - 1)),
                           reads=[Tvaug, Tpt[pi]], writes=[PB[5]])
                        if pending:
                            pending.pop(0)()
                        yield
                        if jc != NKT - 1:
                            continue
                        while pending:
                            pending.pop(0)()
                        for hh in range(2):
                            op("dve", lambda h, hh=hh: h.tensor_copy(out=osb[hh][:], in_=bank(4 + hh)[:, :]),
                               writes=[Tosb[hh], PB[4 + hh]])
                        queue_norm(s)
                        yield

                def ophase(j, b, qset):
                    r0 = b * 512
                    pT = pTs[qset]; TpT = TpTs[qset]
                    aTb = aTs[b % 2]; TaT = TaTs[b % 2]
                    for t in range(4):
                        xb = t % 2
                        b0 = (t % 2) * 2
                        for c in range(8):
                            src_t = aTb if c < 4 else pT
                            Tsrc = TaT if c < 4 else TpT
                            for half in range(2):
                                op("pe", lambda h, c=c, half=half, t=t, b0=b0, src_t=src_t: h.matmul(
                                    bank(b0 + half)[:, :], lhsT=src_t[:, c % 4, t * 128:(t + 1) * 128],
                                    rhs=w_out[:, c, half * 512:(half + 1) * 512], start=(c == 0), stop=(c == 7)),
                                   reads=[Tsrc, Twout], writes=[PB[b0 + half]])
                        ti = (r0 + t * 128) // 128
                        resid_epilogue(b0, None, None, tmpo, Ttmpo,
                                       x1buf[j, r0 + t * 128:r0 + (t + 1) * 128, :], X1T[j][ti], add_x=False)

                def ophase_side(j, b, done):
                    r0 = b * 512
                    pT = pTs[b % 2]; TpT = TpTs[b % 2]
                    aTb = aTs[b % 2]; TaT = TaTs[b % 2]
                    p7 = psb7[:, :].bitcast(F32)
                    while pending:
                        pending.pop(0)()
                    for t in range(4):
                        sst, Ts = next_ss()
                        op("pool", lambda h: h.memset(sst[:, 0:2], 0.0), writes=[Ts])
                        yield
                        for half in range(2):
                            yield
                            for c in range(8):
                                src_t = aTb if c < 4 else pT
                                Tsrc = TaT if c < 4 else TpT
                                op("pe", lambda h, c=c, src_t=src_t: h.matmul(
                                    p7, lhsT=src_t[:, c % 4, t * 128:(t + 1) * 128],
                                    rhs=w_out[:, c, half * 512:(half + 1) * 512], start=(c == 0), stop=(c == 7)),
                                   reads=[Tsrc, Twout], writes=[PB[7]])
                            yield
                            op("act", lambda h: h.activation(out=tmpo[:, half * 512:(half + 1) * 512], in_=p7, func=AF.Square,
                                                             accum_out=sst[:, half:half + 1]),
                               writes=[Ts, Ttmpo, PB[7]])
                            yield
                            op("dve", lambda h: h.tensor_copy(out=tmpo[:, half * 512:(half + 1) * 512], in_=p7),
                               writes=[Ttmpo, PB[7]])
                            yield
                        op("dve", lambda h: h.tensor_tensor(out=sst[:, 0:1], in0=sst[:, 0:1], in1=sst[:, 1:2], op=ALU.add), writes=[Ts])
                        yield
                        rstd_from_ss(sst[:, 0:1], Ts, 128, 1.0 / D)
                        yield
                        op("dve", lambda h: h.scalar_tensor_tensor(out=tmpo, in0=tmpo, scalar=sst[:, 0:1], in1=grow[:],
                                                                   op0=ALU.mult, op1=ALU.mult),
                           reads=[Ts, Tgrow], writes=[Ttmpo])
                        yield
                        ti = (r0 + t * 128) // 128
                        S.dma(x1buf[j, r0 + t * 128:r0 + (t + 1) * 128, :], tmpo, Ttmpo, reads=[Ttmpo], writes=[X1T[j][ti]])
                        yield
                    done[0] = True

                pending = []
                run_interleaved([pphase(j, 0, 0)], 1)
                for b in range(NBLK):
                    gens = [attn(j, b, b % 2)]
                    done = [True]
                    side1 = []
                    side2 = []
                    if b >= 1:
                        done = [False]
                        side2.append(ophase_side(j, b - 1, done))
                    if b + 1 < NBLK:
                        side1.append(pphase(j, b + 1, (b + 1) % 2, None, "q"))
                        side2.append(pphase(j, b + 1, (b + 1) % 2, done, "rest"))
                    if side1 or side2:
                        gens.append(itertools.chain(*(side1 + [interleave_gen(side2, 2)])))
                    run_interleaved(gens, 2)
                while pending:
                    pending.pop(0)()
                ophase(j, NBLK - 1, (NBLK - 1) % 2)
                ooff += SOj
            S.barrier()

        with ExitStack() as st:
            w1 = sb("w1_bf", [128, 8, DFF], BF16, st); Tw1 = Trk()
            w2 = sb("w2_bf", [128, 32, D], BF16, st); Tw2 = Trk()
            with ExitStack() as st2:
                stg = [sb("mstg%d" % i, [128, 2048], F32, st2) for i in range(2)]; Tstg = [Trk(), Trk()]
                load_cast(lambda i: w1[:, i // 2, (i % 2) * 2048:(i % 2 + 1) * 2048],
                          lambda i: w_ff1_d[:, i // 2, (i % 2) * 2048:(i % 2 + 1) * 2048], 16, stg, Tstg, Tw1, 2048)
                load_cast(lambda i: w2[:, 2 * i:2 * i + 2, :].rearrange("p a n -> p (a n)"),
                          lambda i: w_ff2_d[:, 2 * i:2 * i + 2, :].rearrange("p a n -> p (a n)"), 16, stg, Tstg, Tw2, 2048)
                S.barrier()
            xts = [sb("mxt%d" % i, [128, D], F32, st) for i in range(4)]; Txts = [Trk() for _ in range(4)]
            dts = [sb("mdt%d" % i, [128, D], F32, st) for i in range(2)]; Tdts = [Trk(), Trk()]
            xn = [sb("mxn%d" % i, [128, D], BF16, st) for i in range(2)]; Txn = [Trk(), Trk()]
            h2T = sb("h2T", [128, 8, 256], BF16, st); Th2T = [Trk(), Trk()]
            gT = sb("gT", [128, 32, 256], BF16, st); TgT = Trk()
            r32 = [sb("r32_%d" % i, [128, 256], F32, st) for i in range(2)]; Tr32 = [Trk(), Trk()]
            tmpo = rep[:, :, :].rearrange("p a n -> p (a n)"); Ttmpo = Trep

            blocks = [(j, blk) for j, (SQj, SOj) in enumerate(jobs) for blk in range(SQj // 256)]
            xt_i = [0]
            xb_of = {}

            def mlp_norm_a(bi):
                j, blk = blocks[bi]
                r0 = blk * 256
                xbs = []
                gens = []
                for t in range(2):
                    xb = xt_i[0] % 4
                    xt_i[0] += 1
                    xbs.append(xb)
                    ti = (r0 + t * 128) // 128
                    S.dma(xts[xb][:], xq[j, r0 + t * 128:r0 + (t + 1) * 128, :], Txts[xb], writes=[Txts[xb]])
                    S.dma(dts[t][:], x1buf[j, r0 + t * 128:r0 + (t + 1) * 128, :], Tdts[t],
                          reads=[X1T[j][ti]], writes=[Tdts[t]])
                    op("dve", lambda h, xb=xb, t=t: h.tensor_tensor(out=xts[xb][:], in0=xts[xb][:], in1=dts[t][:], op=ALU.add),
                       reads=[Tdts[t]], writes=[Txts[xb]])
                    gens.append(norm_a(xts[xb], Txts[xb], 128, xn[t], Txn[t]))
                xb_of[bi] = xbs
                run_interleaved(gens, 2)

            def mlp_norm_b(bi):
                j, blk = blocks[bi]
                run_interleaved([norm_b(128, j, 3, 4, (lambda k, t=t: h2T[:, k, t * 128:(t + 1) * 128]), Th2T[t], xn[t], Txn[t], t)
                                 for t in range(2)], 2)

            yi = 0
            cur_j = -1
            mlp_norm_a(0)
            mlp_norm_b(0)
            for bi, (j, blk) in enumerate(blocks):
                r0 = blk * 256
                if j != cur_j:
                    build_grow(j, 5)
                    cur_j = j
                for fc in range(32):
                    bk = fc % 2
                    for k in range(8):
                        op("pe", lambda h, k=k, fc=fc, bk=bk: h.matmul(bank(bk)[:, 0:256], lhsT=w1[:, k, fc * 128:(fc + 1) * 128],
                                                                       rhs=h2T[:, k, :], start=(k == 0), stop=(k == 7)),
                           reads=[Tw1] + Th2T, writes=[PB[bk]])
                    op("act", lambda h, bk=bk: h.activation(out=r32[bk][:], in_=bank(bk)[:, 0:256], func=AF.Relu),
                       writes=[Tr32[bk], PB[bk]])
                    op("dve", lambda h, bk=bk, fc=fc: h.tensor_tensor(out=gT[:, fc, :], in0=r32[bk][:], in1=r32[bk][:], op=ALU.mult),
                       reads=[Tr32[bk]], writes=[TgT])
                if bi + 1 < len(blocks):
                    mlp_norm_a(bi + 1)
                for t in range(2):
                    b0 = 2 + (yi % 2) * 2
                    yi += 1
                    for fc in range(32):
                        for half in range(2):
                            op("pe", lambda h, fc=fc, half=half, t=t, b0=b0: h.matmul(
                                bank(b0 + half)[:, :], lhsT=gT[:, fc, t * 128:(t + 1) * 128],
                                rhs=w2[:, fc, half * 512:(half + 1) * 512], start=(fc == 0), stop=(fc == 31)),
                               reads=[TgT, Tw2], writes=[PB[b0 + half]])
                    xb = xb_of[bi][t]
                    resid_epilogue(b0, xts[xb], Txts[xb], tmpo, Ttmpo,
                                   yout[j, r0 + t * 128:r0 + (t + 1) * 128, :], Trk())
                    if t == 0 and bi + 1 < len(blocks):
                        mlp_norm_b(bi + 1)
        S.finish()
    return nc


def _rope_table(pos):
    pos = np.asarray(pos)
    row = (pos // GRID_W).astype(np.float32)
    col = (pos % GRID_W).astype(np.float32)
    nf = 16
    inv = (1.0 / (np.float32(ROPE_THETA) ** (np.arange(nf, dtype=np.float32) / np.float32(nf)))).astype(np.float32)
    ang = np.concatenate([row[:, None] * inv[None, :], col[:, None] * inv[None, :]], axis=-1).astype(np.float32)
    c = np.cos(ang).astype(np.float32)
    s = np.sin(ang).astype(np.float32)
    return np.concatenate([c, c, -s, s], axis=-1).astype(np.float32)


def _pm(v, k):
    return np.ascontiguousarray(np.asarray(v, np.float32).reshape(k, 128).T)


def prep_shared(w_ada, b_ada, g_pre_mix, g_post_mix, g_pre_mlp, g_post_mlp, w_in, g_q, g_k, w_pool,
                pool_scale, w_out, w_ff1, w_ff2):
    f = np.float32
    sh = {}
    sh["w_ada"] = np.ascontiguousarray(np.asarray(w_ada, f).reshape(8, 128, 6144).transpose(1, 0, 2))
    sh["b_adaT"] = _pm(b_ada, 48)
    sh["gvec"] = np.ascontiguousarray(np.concatenate([_pm(g_pre_mix, 8), _pm(g_post_mix, 8), _pm(g_pre_mlp, 8),
                                                      _pm(g_post_mlp, 8)], axis=1))
    gq = np.asarray(g_q, f)
    gk = np.asarray(g_k, f)
    sh["gqk"] = np.ascontiguousarray(np.broadcast_to(np.concatenate([np.tile(gq, 8), np.tile(gk, 2)])[None, :], (128, 640)))
    w_in = np.asarray(w_in, f)
    qperm = np.concatenate([np.concatenate([np.arange(s * 64, (s + 1) * 64), np.arange((s + 4) * 64, (s + 5) * 64)])
                            for s in range(4)])
    cols = np.concatenate([qperm, np.arange(512, DIN)])
    sh["w_in"] = np.ascontiguousarray(w_in[:, cols].reshape(8, 128, DIN).transpose(1, 0, 2))
    w_out = np.asarray(w_out, f)
    rows = np.concatenate([qperm, np.arange(512, D)])
    sh["w_out"] = np.ascontiguousarray(w_out[rows, :].reshape(8, 128, D).transpose(1, 0, 2))
    sh["w_pool"] = np.ascontiguousarray(np.asarray(w_pool, f).transpose(1, 0, 2))
    sh["pscale"] = _pm(pool_scale, 4)
    sh["w_ff1"] = np.ascontiguousarray(np.asarray(w_ff1, f).reshape(8, 128, DFF).transpose(1, 0, 2))
    sh["w_ff2"] = np.ascontiguousarray(np.asarray(w_ff2, f).reshape(32, 128, D).transpose(1, 0, 2))
    return sh


def prep_core(core_jobs, jobs):
    f = np.float32
    NJ = len(jobs)
    SQ = jobs[0][0]
    NB = SQ // 512
    SOT = sum(j[1] for j in jobs)
    SOA = max(SOT, 128)
    xq = np.empty((NJ, SQ, D), f)
    xo = np.zeros((SOA, D), f)
    xh = np.zeros((NJ, NB, 16, D), f)
    ropeq = np.empty((NJ, SQ, 128), f)
    ropeo = np.zeros((SOA, 128), f)
    hmask = np.zeros((NJ, NB, 16), f)
    icnt = np.zeros((NJ, 2, 4, 8), f)
    cT = np.empty((128, 8, NJ), f)
    ooff = 0
    for j, ((xs, c, o0), (SQj, SOj)) in enumerate(zip(core_jobs, jobs)):
        Sseq = xs.shape[0]
        assert Sseq == SQj + SOj
        xq[j] = xs[o0:o0 + SQj]
        ropeq[j] = _rope_table(np.arange(o0, o0 + SQj))
        if SOj:
            opos = np.concatenate([np.arange(0, o0), np.arange(o0 + SQj, Sseq)])
            xo[ooff:ooff + SOj] = xs[opos]
            ropeo[ooff:ooff + SOj] = _rope_table(opos)
            ooff += SOj
        for b in range(NB):
            for e in range(16):
                t = o0 + b * 512 - 8 + e if e < 8 else o0 + (b + 1) * 512 + (e - 8)
                if 0 <= t < Sseq:
                    xh[j, b, e] = xs[t]
                    hmask[j, b, e] = 1.0
        for edge in range(2):
            for g, w in enumerate(POOL_W):
                for e in range(8):
                    t = o0 + e if edge == 0 else o0 + SQj - 8 + e
                    lo = min(max(t - w // 2, 0), Sseq)
                    hi = min(max(t - w // 2 + w, 0), Sseq)
                    icnt[j, edge, g, e] = f(1.0) / f(hi - lo)
        cT[:, :, j] = np.asarray(c, f).reshape(8, 128).T
    m = {
        "xq": xq, "xo": xo, "xh": xh, "ropeq": ropeq, "ropeo": ropeo,
        "hmask": np.ascontiguousarray(np.broadcast_to(hmask.reshape(1, -1), (128, NJ * NB * 16))),
        "icnt": np.ascontiguousarray(np.broadcast_to(icnt.reshape(1, -1), (128, NJ * 2 * 32))),
        "cT": np.ascontiguousarray(cT.reshape(128, 8 * NJ)),
    }
    return m


_NC_CACHE = {}


def kernel(x_prompt, x_sample, c_prompt, c_sample, w_ada, b_ada, g_pre_mix, g_post_mix, g_pre_mlp, g_post_mlp,
           w_in, g_q, g_k, w_pool, pool_scale, w_out, w_ff1, w_ff2):
    f = np.float32
    x_prompt = np.asarray(x_prompt, f)
    x_sample = np.asarray(x_sample, f)
    c_prompt = np.asarray(c_prompt, f)
    c_sample = np.asarray(c_sample, f)
    B, Sp, _ = x_prompt.shape
    Bs, Ss, _ = x_sample.shape
    assert B == 2 * N_CORES and Bs * 2 == N_CORES and Ss == 2 * Sp
    jobs = [(Sp, 0), (Sp, 0), (Sp, Sp)]
    sh = prep_shared(w_ada[0], b_ada[0], g_pre_mix[0], g_post_mix[0], g_pre_mlp[0], g_post_mlp[0], w_in[0],
                     g_q[0], g_k[0], w_pool[0], pool_scale[0], w_out[0], w_ff1[0], w_ff2[0])
    in_maps = []
    for c in range(N_CORES):
        cj = [(x_prompt[2 * c], c_prompt[2 * c], 0), (x_prompt[2 * c + 1], c_prompt[2 * c + 1], 0),
              (x_sample[c // 2], c_sample[c // 2], (c % 2) * Sp)]
        m = prep_core(cj, jobs)
        m.update(sh)
        in_maps.append(m)
    key = tuple(jobs)
    if key not in _NC_CACHE:
        _NC_CACHE[key] = build_program(jobs)
    nc = _NC_CACHE[key]
    res = run_bass_kernel_spmd(nc, in_maps, core_ids=list(range(N_CORES)))
    y_prompt = np.empty((B, Sp, D), f)
    y_sample = np.empty((Bs, Ss, D), f)
    for c in range(N_CORES):
        y = np.asarray(res.results[c]["y"], f)
        y_prompt[2 * c] = y[0]
        y_prompt[2 * c + 1] = y[1]
        y_sample[c // 2, (c % 2) * Sp:(c % 2 + 1) * Sp] = y[2]
    return (y_prompt, y_sample)
```

```python
from contextlib import ExitStack
import itertools
import numpy as np
import concourse.bass as bass
import concourse.mybir as mybir
from concourse.bass_utils import run_bass_kernel_spmd

F32 = mybir.dt.float32
BF16 = mybir.dt.bfloat16
AF = mybir.ActivationFunctionType
ALU = mybir.AluOpType
AX = mybir.AxisListType

D = 1024
DIN = 1280
DFF = 4096
EPS = 1e-6
POOL_W = (2, 4, 8, 16)
GRID_W = 64
ROPE_THETA = 10000.0
N_CORES = 8


class Trk:
    __slots__ = ("w", "r", "sem", "cnt", "name")

    def __init__(self, name=""):
        self.w = None
        self.r = {}
        self.sem = None
        self.cnt = 0
        self.name = name


class Sched:
    def __init__(self, nc):
        self.nc = nc
        self.eng = {}
        for name, h in (("pe", nc.tensor), ("act", nc.scalar), ("dve", nc.vector),
                        ("pool", nc.gpsimd), ("sp", nc.sync)):
            self.eng[name] = dict(h=h, sem=nc.alloc_semaphore("s_" + name), cnt=0, waited={})
        self.dma_trks = []
        self.nsem = 0

    def _deps(self, reads, writes):
        deps = {}
        for t in reads:
            if t.w is not None:
                k, s, v = t.w
                if k not in deps or deps[k][1] < v:
                    deps[k] = (s, v)
        for t in writes:
            if t.w is not None:
                k, s, v = t.w
                if k not in deps or deps[k][1] < v:
                    deps[k] = (s, v)
            for k, (s, v) in t.r.items():
                if k not in deps or deps[k][1] < v:
                    deps[k] = (s, v)
        return deps

    def _wait(self, ename, deps):
        e = self.eng[ename]
        for k, (s, v) in deps.items():
            if k == "pe" and ename == "pe":
                continue
            if e["waited"].get(k, 0) < v:
                e["h"].wait_ge(s, v)
                e["waited"][k] = v

    def op(self, ename, fn, reads=(), writes=()):
        e = self.eng[ename]
        self._wait(ename, self._deps(reads, writes))
        inst = fn(e["h"])
        e["cnt"] += 1
        inst.then_inc(e["sem"], 1)
        rec = (ename, e["sem"], e["cnt"])
        for t in writes:
            t.w = rec
            t.r = {}
        for t in reads:
            t.r[ename] = (e["sem"], e["cnt"])
        return inst

    def dma(self, out, in_, strk, reads=(), writes=()):
        e = self.eng["sp"]
        self._wait("sp", self._deps(reads, writes))
        if strk.sem is None:
            strk.sem = self.nc.alloc_semaphore("d%d" % self.nsem)
            self.nsem += 1
            self.dma_trks.append(strk)
        inst = e["h"].dma_start(out=out, in_=in_)
        strk.cnt += 16
        inst.then_inc(strk.sem, 16)
        key = "d:%d" % id(strk)
        rec = (key, strk.sem, strk.cnt)
        for t in writes:
            t.w = rec
            t.r = {}
        for t in reads:
            t.r[key] = (strk.sem, strk.cnt)
        return inst

    def barrier(self):
        for en, e in self.eng.items():
            for on, o in self.eng.items():
                if on == en or on == "sp" or o["cnt"] == 0:
                    continue
                if e["waited"].get(on, 0) < o["cnt"]:
                    e["h"].wait_ge(o["sem"], o["cnt"])
                    e["waited"][on] = o["cnt"]
            for t in self.dma_trks:
                k = "d:%d" % id(t)
                if t.cnt and e["waited"].get(k, 0) < t.cnt:
                    e["h"].wait_ge(t.sem, t.cnt)
                    e["waited"][k] = t.cnt

    def finish(self):
        e = self.eng["sp"]
        for t in self.dma_trks:
            e["h"].wait_ge(t.sem, t.cnt)


def run_interleaved(gens, width):
    it = iter(gens)
    active = []
    more = True
    while True:
        while more and len(active) < width:
            try:
                active.append(next(it))
            except StopIteration:
                more = False
        if not active:
            break
        for g in list(active):
            try:
                next(g)
            except StopIteration:
                active.remove(g)


def interleave_gen(gens, width):
    it = iter(gens)
    active = []
    more = True
    while True:
        while more and len(active) < width:
            try:
                active.append(next(it))
            except StopIteration:
                more = False
        if not active:
            return
        for g in list(active):
            try:
                next(g)
            except StopIteration:
                active.remove(g)
        yield


def multi(g, k):
    while True:
        for _ in range(k):
            try:
                next(g)
            except StopIteration:
                return
        yield


def build_program(jobs):
    NJ = len(jobs)
    SQ = jobs[0][0]
    assert all(j[0] == SQ for j in jobs)
    NB = SQ // 512
    SOT = sum(j[1] for j in jobs)
    SOA = max(SOT, 128)
    NKT_MAX = max((j[0] + j[1]) // 128 for j in jobs)

    nc = bass.Bass("TRN2", target_bir_lowering=False)

    def din(name, shape):
        return nc.dram_tensor(name, shape, F32, kind="ExternalInput").ap()

    xq = din("xq", [NJ, SQ, D])
    xo = din("xo", [SOA, D])
    xh = din("xh", [NJ, NB, 16, D])
    ropeq = din("ropeq", [NJ, SQ, 128])
    ropeo = din("ropeo", [SOA, 128])
    hmask_d = din("hmask", [128, NJ * NB * 16])
    icnt_d = din("icnt", [128, NJ * 2 * 32])
    cT_d = din("cT", [128, 8 * NJ])
    w_ada_d = din("w_ada", [128, 8, 6144])
    b_adaT_d = din("b_adaT", [128, 48])
    gvec_d = din("gvec", [128, 32])
    gqk_d = din("gqk", [128, 640])
    w_in_d = din("w_in", [128, 8, DIN])
    w_out_d = din("w_out", [128, 8, D])
    w_pool_d = din("w_pool", [128, 4, 128])
    pscale_d = din("pscale", [128, 4])
    w_ff1_d = din("w_ff1", [128, 8, DFF])
    w_ff2_d = din("w_ff2", [128, 32, D])
    yout = nc.dram_tensor("y", [NJ, SQ, D], F32, kind="ExternalOutput").ap()
    x1buf = nc.dram_tensor("x1buf", [NJ, SQ, D], F32).ap()

    S = Sched(nc)
    op = S.op

    ps = nc.alloc_psum_tensor("ps", [128, 7 * 512], F32)
    psb7 = nc.alloc_psum_tensor("psb7", [128, 1024], BF16)
    psbs = [psb7[:, :]] + [ps[:, i * 512:(i + 1) * 512].bitcast(BF16) for i in (6, 5, 4)]
    PB = [Trk("bank%d" % i) for i in range(8)]
    PBB = [PB[7], PB[6], PB[5], PB[4]]

    def bank(i, n=512, parts=slice(0, 128)):
        return ps[parts, i * 512:i * 512 + n]

    with ExitStack() as glob:
        def sb(name, shape, dt=F32, stack=glob):
            return stack.enter_context(nc.sbuf_tensor("sb_" + name, shape, dt))

        ident = sb("ident", [128, 128], BF16); Tident = Trk()
        identf = sb("identf", [128, 128], F32); Tidentf = Trk()
        eps_t = sb("eps_t", [128, 1]); Teps = Trk()
        modD = sb("modD", [128, 6, 8, NJ]); TmodD = Trk()
        gqk = sb("gqk", [128, 640]); Tgqk = Trk()
        hmask = sb("hmask", [128, NJ * NB * 16]); Thmask = Trk()
        icnt = sb("icnt", [128, NJ * 2 * 32]); Ticnt = Trk()
        pscale = sb("pscale", [128, 4]); Tpscale = Trk()
        grow = sb("grow", [128, D]); Tgrow = Trk()
        rep = sb("rep", [128, 8, 128]); Trep = Trk()
        NSS = 8
        ss_t = [sb("ss%d" % i, [128, 2]) for i in range(NSS)]; Tss = [Trk() for _ in range(NSS)]
        ss_i = [0]

        for t_, tr_ in ((ident, Tident), (identf, Tidentf)):
            op("pool", lambda h, t_=t_: h.memset(t_[:], 0.0), writes=[tr_])
            op("pool", lambda h, t_=t_: h.affine_select(out=t_[:], in_=t_[:], pattern=[[-1, 128]],
                                                       compare_op=ALU.not_equal, fill=1.0, base=0,
                                                       channel_multiplier=1), writes=[tr_])
        op("pool", lambda h: h.memset(eps_t[:], EPS), writes=[Teps])
        S.dma(gqk[:], gqk_d[:, :], Tgqk, writes=[Tgqk])
        S.dma(hmask[:], hmask_d[:, :], Thmask, writes=[Thmask])
        S.dma(icnt[:], icnt_d[:, :], Ticnt, writes=[Ticnt])
        S.dma(pscale[:], pscale_d[:, :], Tpscale, writes=[Tpscale])

        with ExitStack() as st:
            wa = [sb("wa%d" % i, [128, 8, 1024], F32, st) for i in range(2)]; Twa = [Trk(), Trk()]
            cT = sb("cT", [128, 8 * NJ], F32, st); TcT = Trk()
            e_t = sb("e_t", [128, 8 * NJ], F32, st); Te = Trk()
            sT = sb("sT", [128, 8 * NJ], F32, st); TsT = Trk()
            b_adaT = sb("b_adaT", [128, 48], F32, st); Tb = Trk()
            gvec = sb("gvec", [128, 32], F32, st); Tg = Trk()
            modT = sb("modT", [128, 6, 8, NJ], F32, st); TmodT = Trk()
            S.dma(cT[:], cT_d[:, :], TcT, writes=[TcT])
            S.dma(b_adaT[:], b_adaT_d[:, :], Tb, writes=[Tb])
            S.dma(gvec[:], gvec_d[:, :], Tg, writes=[Tg])
            op("act", lambda h: h.activation(out=e_t[:], in_=cT[:], func=AF.Exp, scale=-1.0),
               reads=[TcT], writes=[Te])
            op("dve", lambda h: h.tensor_scalar_add(out=e_t[:], in0=e_t[:], scalar1=1.0), writes=[Te])
            op("dve", lambda h: h.reciprocal(out=e_t[:], in_=e_t[:]), writes=[Te])
            op("dve", lambda h: h.tensor_mul(out=sT[:], in0=cT[:], in1=e_t[:]), reads=[TcT, Te], writes=[TsT])
            sT3 = sT[:, :].rearrange("p (k j) -> p k j", j=NJ)
            for blk in range(6):
                b = blk % 2
                S.dma(wa[b][:], w_ada_d[:, :, blk * 1024:(blk + 1) * 1024], Twa[b], writes=[Twa[b]])
                for mc in range(8):
                    for k in range(8):
                        op("pe", lambda h, b=b, mc=mc, k=k: h.matmul(
                            bank(b)[:, mc * NJ:(mc + 1) * NJ], lhsT=wa[b][:, k, mc * 128:(mc + 1) * 128],
                            rhs=sT3[:, k, :], start=(k == 0), stop=(k == 7)),
                           reads=[Twa[b], TsT], writes=[PB[b]])
                op("dve", lambda h, b=b, blk=blk: h.tensor_tensor(
                    out=modT[:, blk, :, :], in0=bank(b)[:, 0:8 * NJ].rearrange("p (m j) -> p m j", j=NJ),
                    in1=b_adaT[:, blk * 8:(blk + 1) * 8].unsqueeze(2).broadcast_to([128, 8, NJ]), op=ALU.add),
                   reads=[Tb], writes=[TmodT, PB[b]])

            def gb(i):
                return gvec[:, i * 8:(i + 1) * 8].unsqueeze(2).broadcast_to([128, 8, NJ])
            op("dve", lambda h: h.scalar_tensor_tensor(out=modD[:, 0], in0=modT[:, 1], scalar=1.0, in1=gb(0),
                                                       op0=ALU.add, op1=ALU.mult), reads=[TmodT, Tg], writes=[TmodD])
            op("dve", lambda h: h.tensor_copy(out=modD[:, 1], in_=modT[:, 0]), reads=[TmodT], writes=[TmodD])
            op("dve", lambda h: h.tensor_tensor(out=modD[:, 2], in0=modT[:, 2], in1=gb(1), op=ALU.mult),
               reads=[TmodT, Tg], writes=[TmodD])
            op("dve", lambda h: h.scalar_tensor_tensor(out=modD[:, 3], in0=modT[:, 4], scalar=1.0, in1=gb(2),
                                                       op0=ALU.add, op1=ALU.mult), reads=[TmodT, Tg], writes=[TmodD])
            op("dve", lambda h: h.tensor_copy(out=modD[:, 4], in_=modT[:, 3]), reads=[TmodT], writes=[TmodD])
            op("dve", lambda h: h.tensor_tensor(out=modD[:, 5], in0=modT[:, 5], in1=gb(3), op=ALU.mult),
               reads=[TmodT, Tg], writes=[TmodD])
            S.barrier()

        def build_grow(j, which):
            op("dve", lambda h: h.tensor_copy(out=rep[:], in_=modD[:, which, :, j].unsqueeze(2).broadcast_to([128, 8, 128])),
               reads=[TmodD], writes=[Trep])
            for k in range(8):
                bk = 4 + k // 4
                op("pe", lambda h, k=k, bk=bk: h.transpose(out=bank(bk)[:, (k % 4) * 128:(k % 4 + 1) * 128],
                                                          in_=rep[:, k, :], identity=identf[:]),
                   reads=[Trep, Tidentf], writes=[PB[bk]])
            for half in range(2):
                op("act", lambda h, half=half: h.activation(out=grow[:, half * 512:(half + 1) * 512],
                                                            in_=bank(4 + half)[:, :], func=AF.Copy),
                   writes=[Tgrow, PB[4 + half]])

        def rstd_from_ss(ssv, Tssv, rows, inv_n):
            op("act", lambda h: h.activation(out=ssv, in_=ssv, func=AF.Ln, scale=inv_n, bias=eps_t[0:rows, 0:1]),
               reads=[Teps], writes=[Tssv])
            op("act", lambda h: h.activation(out=ssv, in_=ssv, func=AF.Exp, scale=-0.5), writes=[Tssv])

        def next_ss():
            i = ss_i[0] % NSS
            ss_i[0] += 1
            return ss_t[i], Tss[i]

        def norm_a(xt_, Txt, rows, xn, Txn, ss_ext=None):
            sst, Ts = ss_ext if ss_ext is not None else next_ss()
            op("pool", lambda h: h.memset(sst[0:rows, 0:1], 0.0), writes=[Ts])
            yield
            op("act", lambda h: h.activation(out=xn[0:rows, :], in_=xt_[0:rows, :], func=AF.Square,
                                             accum_out=sst[0:rows, 0:1]),
               reads=[Txt], writes=[Ts, Txn])
            yield
            rstd_from_ss(sst[0:rows, 0:1], Ts, rows, 1.0 / D)
            yield
            op("dve", lambda h: h.tensor_scalar_mul(out=xn[0:rows, :], in0=xt_[0:rows, :], scalar1=sst[0:rows, 0:1]),
               reads=[Txt, Ts], writes=[Txn])
            yield

        def norm_b(rows, j, gsi, shi, hT_dst, Thd, xn, Txn, pi, act_split=False):
            psb = psbs[pi]
            for k in range(8):
                op("pe", lambda h, k=k: h.transpose(out=psb[:, k * rows:(k + 1) * rows],
                                                    in_=xn[0:rows, k * 128:(k + 1) * 128],
                                                    identity=ident[0:rows, 0:rows]),
                   reads=[Txn, Tident], writes=[PBB[pi]])
            yield
            for k in range(8):
                if act_split and k % 2 == 1:
                    op("act", lambda h, k=k: h.activation(out=hT_dst(k), in_=psb[:, k * rows:(k + 1) * rows], func=AF.Identity,
                                                          scale=modD[:, gsi, k, j:j + 1], bias=modD[:, shi, k, j:j + 1]),
                       reads=[TmodD], writes=[Thd, PBB[pi]])
                else:
                    op("dve", lambda h, k=k: h.tensor_scalar(out=hT_dst(k), in0=psb[:, k * rows:(k + 1) * rows],
                                                             scalar1=modD[:, gsi, k, j:j + 1], scalar2=modD[:, shi, k, j:j + 1],
                                                             op0=ALU.mult, op1=ALU.add),
                       reads=[TmodD], writes=[Thd, PBB[pi]])
                if k % 2 == 1:
                    yield

        def resid_epilogue(b0, xt_, Txt, tmp, Ttmp, dst_ap, Tdst, add_x=True):
            sst, Ts = next_ss()
            mv = ps[:, b0 * 512:b0 * 512 + 1024]
            op("pool", lambda h: h.memset(sst[:, 0:1], 0.0), writes=[Ts])
            op("act", lambda h: h.activation(out=tmp, in_=mv, func=AF.Square, accum_out=sst[:, 0:1]),
               writes=[Ts, Ttmp, PB[b0], PB[b0 + 1]])
            rstd_from_ss(sst[:, 0:1], Ts, 128, 1.0 / D)
            op("dve", lambda h: h.scalar_tensor_tensor(out=tmp, in0=mv, scalar=sst[:, 0:1], in1=grow[:],
                                                       op0=ALU.mult, op1=ALU.mult),
               reads=[Ts, Tgrow], writes=[Ttmp, PB[b0], PB[b0 + 1]])
            if add_x:
                op("pool", lambda h: h.tensor_tensor(out=xt_[:], in0=xt_[:], in1=tmp, op=ALU.add),
                   reads=[Ttmp], writes=[Txt])
                S.dma(dst_ap, xt_[:], Txt, reads=[Txt], writes=[Tdst])
            else:
                S.dma(dst_ap, tmp, Ttmp, reads=[Ttmp], writes=[Tdst])

        def load_cast(dst_view_fn, src_view_fn, nchunks, stg, Tstg, Tdst, width):
            for i in range(nchunks):
                b = i % 2
                S.dma(stg[b][:, 0:width], src_view_fn(i), Tstg[b], writes=[Tstg[b]])
                eng = "pool" if i % 2 == 0 else "dve"
                op(eng, lambda h, i=i, b=b: h.tensor_copy(out=dst_view_fn(i), in_=stg[b][:, 0:width]),
                   reads=[Tstg[b]], writes=[Tdst])

        X1T = [[Trk() for _ in range(SQ // 128)] for _ in range(NJ)]
        with ExitStack() as st:
            w_in = sb("w_in_bf", [128, 8, DIN], BF16, st); Twin = Trk()
            w_out = sb("w_out_bf", [128, 8, D], BF16, st); Twout = Trk()
            w_pool = sb("w_pool_bf", [128, 4, 128], BF16, st); Twpool = Trk()
            kT = sb("kT", [128, NKT_MAX * 128], BF16, st); TkT = Trk()
            rstd_all = sb("rstd_all", [128, SQ // 128, 2], F32, st); Trstd = [Trk() for _ in range(SQ // 128)]
            vaug = sb("vaug", [128, NKT_MAX, 2, 128], BF16, st); Tvaug = Trk()
            with ExitStack() as st2:
                stg = [sb("stg%d" % i, [128, 2560], F32, st2) for i in range(2)]; Tstg = [Trk(), Trk()]
                load_cast(lambda i: w_in[:, 2 * i:2 * i + 2, :].rearrange("p a n -> p (a n)"),
                          lambda i: w_in_d[:, 2 * i:2 * i + 2, :].rearrange("p a n -> p (a n)"), 4, stg, Tstg, Twin, 2560)
                load_cast(lambda i: w_out[:, 2 * i:2 * i + 2, :].rearrange("p a n -> p (a n)"),
                          lambda i: w_out_d[:, 2 * i:2 * i + 2, :].rearrange("p a n -> p (a n)"), 4, stg, Tstg, Twout, 2048)
                load_cast(lambda i: w_pool[:, :, :].rearrange("p a n -> p (a n)"),
                          lambda i: w_pool_d[:, :, :].rearrange("p a n -> p (a n)"), 1, stg, Tstg, Twpool, 512)
                S.barrier()
            xts = [sb("xt%d" % i, [128, D], F32, st) for i in range(2)]; Txts = [Trk(), Trk()]
            xn = [sb("xn%d" % i, [128, D], BF16, st) for i in range(2)]; Txn = [Trk(), Trk()]
            hTt = [sb("hTt%d" % i, [128, 8, 128], BF16, st) for i in range(2)]; ThTt = [Trk(), Trk()]
            hTb = sb("hTb", [128, 8, 512], BF16, st); ThTb = [Trk() for _ in range(4)]
            hTh = sb("hTh", [128, 8, 16], BF16, st); ThTh = Trk()
            ropet = sb("ropet", [128, 4, 128], F32, st); Tropet = Trk()
            ropek = [sb("ropek%d" % i, [128, 128], F32, st) for i in range(2)]; Tropek = [Trk(), Trk()]
            qtmp0 = sb("qtmp0", [128, 4, 512], F32, st); Tqtmp0 = [Trk() for _ in range(4)]
            Wt = sb("Wt", [128, 4, 512], F32, st); TWt = [Trk() for _ in range(4)]
            ktmp = [sb("ktmp%d" % i, [128, 4, 128], F32, st) for i in range(2)]; Tktmp = [[Trk() for _ in range(4)] for _ in range(2)]
            qss = [sb("qss%d" % i, [128, 8], F32, st) for i in range(2)]; Tqss = [Trk(), Trk()]
            qr = [sb("qr%d" % i, [128, 512], BF16, st) for i in range(2)]; Tqr = [Trk(), Trk()]
            qTbs = [sb("qTb%d" % i, [128, 4, 512], BF16, st) for i in range(2)]; TqTbs = [Trk(), Trk()]
            uT = sb("uT", [128, 4, 528], F32, st); TuT = Trk()
            sA = sb("sA", [128, 528], F32, st); TsA = Trk()
            sB = sb("sB", [128, 528], F32, st); TsB = Trk()
            sC = sb("sC", [128, 528], F32, st); TsC = Trk()
            sD_ = sb("sD", [128, 528], F32, st); TsD = Trk()
            tmp8 = sb("tmp8", [128, 4, 8], F32, st); Ttmp8 = Trk()
            mixT = sb("mixT", [128, 4, 512], BF16, st); TmixT = Trk()
            aTs = [sb("aT%d" % i, [128, 4, 512], BF16, st) for i in range(2)]; TaTs = [Trk(), Trk()]
            aT = aTs[0]
            pTs = [sb("pT%d" % i, [128, 4, 512], BF16, st) for i in range(2)]; TpTs = [Trk(), Trk()]
            pt = [sb("pt%d" % i, [128, 1024], BF16, st) for i in range(3)]; Tpt = [Trk() for _ in range(3)]
            recip = sb("recip", [128, 512], F32, st); Trecip = Trk()
            osb = [sb("osb%d" % i, [128, 512], F32, st) for i in range(2)]; Tosb = [Trk(), Trk()]
            tmpo = rep[:, :, :].rearrange("p a n -> p (a n)"); Ttmpo = Trep
            op("pool", lambda h: h.memset(vaug[:], 1.0), writes=[Tvaug])
            kr_x = [sb("krx%d" % i, [128, 128], BF16, st) for i in range(2)]
            qss_x = [sb("qssx%d" % i, [128, 8], F32, st) for i in range(2)]
            K_xts = xts + [Wt[:, 0:2, :].rearrange("p a n -> p (a n)"), Wt[:, 2:4, :].rearrange("p a n -> p (a n)")]
            K_Txts = Txts + [Trk(), Trk()]
            K_xn = xn + [aT[:, 0:2, :].rearrange("p a n -> p (a n)"), aT[:, 2:4, :].rearrange("p a n -> p (a n)")]
            K_Txn = Txn + [Trk(), Trk()]
            K_hTt = hTt + [mixT[:, 0:2, :].rearrange("p a (k c) -> p (a k) c", c=128),
                           mixT[:, 2:4, :].rearrange("p a (k c) -> p (a k) c", c=128)]
            K_ThTt = ThTt + [Trk(), Trk()]
            K_ktmp = ktmp + [sA[:, 0:512].rearrange("p (i c) -> p i c", c=128), sB[:, 0:512].rearrange("p (i c) -> p i c", c=128)]
            K_Tktmp = Tktmp + [[Trk() for _ in range(4)] for _ in range(2)]
            K_ropek = ropek + [sC[:, 0:128], sD_[:, 0:128]]
            K_Tropek = Tropek + [Trk(), Trk()]
            K_qss = qss + qss_x
            K_Tqss = Tqss + [Trk(), Trk()]
            K_qr = [qr[0][:, 0:128], qr[1][:, 0:128], kr_x[0][:, :], kr_x[1][:, :]]
            K_Tqr = Tqr + [Trk(), Trk()]

            def qk_process(psv, Tbank, nh, gsl, ropeC, ropeS, Trope, out_bf, Tout, tmps, Ttmps, qs, Tqs):
                n = nh * 64
                v3 = lambda a: a.rearrange("p (h d) -> p h d", d=64)
                qsq, qxn, qt1, qt2 = tmps
                Tqsq, Tqxn, Tqt1, Tqt2 = Ttmps
                op("act", lambda h: h.activation(out=qsq, in_=psv, func=AF.Square), writes=[Tqsq, Tbank])
                yield
                op("dve", lambda h: h.tensor_reduce(out=qs[:, 0:nh], in_=v3(qsq), axis=AX.X, op=ALU.add),
                   reads=[Tqsq], writes=[Tqs])
                yield
                rstd_from_ss(qs[:, 0:nh], Tqs, 128, 1.0 / 64)
                yield
                op("dve", lambda h: h.tensor_tensor(out=v3(qxn), in0=v3(psv),
                                                    in1=qs[:, 0:nh].unsqueeze(2).broadcast_to([128, nh, 64]), op=ALU.mult),
                   reads=[Tqs], writes=[Tqxn, Tbank])
                yield
                op("pool", lambda h: h.tensor_tensor(out=qxn, in0=qxn, in1=gsl, op=ALU.mult),
                   reads=[Tgqk], writes=[Tqxn])
                yield
                x3 = v3(qxn)
                op("dve", lambda h: h.tensor_tensor(out=v3(qt1), in0=x3,
                                                    in1=ropeC.unsqueeze(1).broadcast_to([128, nh, 64]), op=ALU.mult),
                   reads=[Tqxn, Trope], writes=[Tqt1])
                t23 = v3(qt2)
                op("pool", lambda h: h.tensor_tensor(out=t23[:, :, 0:32], in0=x3[:, :, 32:64],
                                                     in1=ropeS[:, 0:32].unsqueeze(1).broadcast_to([128, nh, 32]), op=ALU.mult),
                   reads=[Tqxn, Trope], writes=[Tqt2])
                yield
                op("pool", lambda h: h.tensor_tensor(out=t23[:, :, 32:64], in0=x3[:, :, 0:32],
                                                     in1=ropeS[:, 32:64].unsqueeze(1).broadcast_to([128, nh, 32]), op=ALU.mult),
                   reads=[Tqxn, Trope], writes=[Tqt2])
                yield
                op("dve", lambda h: h.tensor_tensor(out=out_bf, in0=qt1, in1=qt2, op=ALU.add),
                   reads=[Tqt1, Tqt2], writes=[Tout])
                yield

            def kv_tile(j, kt, src, rsrc, si, nsrc):
                if kt < 4:
                    S.dma(K_xts[si][:], src, K_Txts[si], writes=[K_Txts[si]])
                S.dma(K_ropek[si][:], rsrc, K_Tropek[si], writes=[K_Tropek[si]])
                yield
                yield from norm_a(K_xts[si], K_Txts[si], 128, K_xn[si], K_Txn[si],
                                  (rstd_all[:, kt, :], Trstd[kt]) if kt < SQ // 128 else None)
                if nsrc is not None:
                    S.dma(K_xts[si][:], nsrc, K_Txts[si], writes=[K_Txts[si]])
                yield from norm_b(128, j, 0, 1, lambda k: K_hTt[si][:, k, :], K_ThTt[si], K_xn[si], K_Txn[si], si, act_split=True)
                for k in range(8):
                    op("pe", lambda h, k=k: h.matmul(bank(si)[:, 0:256], lhsT=K_hTt[si][:, k, :], rhs=w_in[:, k, 512:768],
                                                     start=(k == 0), stop=(k == 7)),
                       reads=[K_ThTt[si], Twin], writes=[PB[si]])
                yield
                op("act", lambda h: h.activation(out=vaug[:, kt, 0, 0:64], in_=bank(si)[:, 128:192], func=AF.Copy),
                   writes=[Tvaug, PB[si]])
                op("act", lambda h: h.activation(out=vaug[:, kt, 1, 64:128], in_=bank(si)[:, 192:256], func=AF.Copy),
                   writes=[Tvaug, PB[si]])
                yield
                yield from qk_process(bank(si)[:, 0:128], PB[si], 2, gqk[:, 512:640], K_ropek[si][:, 0:64], K_ropek[si][:, 64:128],
                                      K_Tropek[si], K_qr[si], K_Tqr[si],
                                      [K_ktmp[si][:, i, :] for i in range(4)], K_Tktmp[si], K_qss[si], K_Tqss[si])
                op("pe", lambda h: h.transpose(out=psbs[si][:, 0:128], in_=K_qr[si], identity=ident[:]),
                   reads=[K_Tqr[si], Tident], writes=[PBB[si]])
                yield
                op("dve", lambda h: h.tensor_copy(out=kT[:, kt * 128:(kt + 1) * 128], in_=psbs[si][:, 0:128]),
                   writes=[TkT, PBB[si]])
                yield

            def q_tile(j, r0, t, si, qset):
                qTb = qTbs[qset]; TqTb = TqTbs[qset]
                pi = 1 - si
                fb = bank(6)[:, :] if si == 0 else psb7[:, :].bitcast(F32)
                Tfb = PB[6] if si == 0 else PB[7]
                ti_ = (r0 + t * 128) // 128
                op("dve", lambda h: h.tensor_scalar_mul(out=xn[si][:, :], in0=xts[si][:, :], scalar1=rstd_all[:, ti_, 0:1]),
                   reads=[Txts[si], Trstd[ti_]], writes=[Txn[si]])
                yield
                if t + 2 < 4:
                    S.dma(xts[si][:], xq[j, r0 + (t + 2) * 128:r0 + (t + 3) * 128, :], Txts[si], writes=[Txts[si]])
                yield
                yield from norm_b(128, j, 0, 1, lambda k: hTb[:, k, t * 128:(t + 1) * 128], ThTb[t], xn[si], Txn[si], pi)
                yield
                for k in range(8):
                    op("pe", lambda h, k=k: h.matmul(fb, lhsT=hTb[:, k, t * 128:(t + 1) * 128],
                                                     rhs=w_in[:, k, 0:512], start=(k == 0), stop=(k == 7)),
                       reads=[ThTb[t], Twin], writes=[Tfb])
                yield
                if si == 0:
                    tm = [qtmp0[:, i, :] for i in range(4)]; Ttm = Tqtmp0
                else:
                    tm = [Wt[:, i, :] for i in range(4)]; Ttm = TWt
                yield from qk_process(fb, Tfb, 8, gqk[:, 0:512], ropet[:, t, 0:64], ropet[:, t, 64:128], Tropet,
                                      qr[si][:, :], Tqr[si], tm, Ttm, qss[si], Tqss[si])
                yield
                for s in range(4):
                    op("pe", lambda h, s=s: h.transpose(out=psbs[pi][:, s * 128:(s + 1) * 128], in_=qr[si][:, s * 128:(s + 1) * 128],
                                                        identity=ident[:]),
                       reads=[Tqr[si], Tident], writes=[PBB[pi]])
                yield
                op("dve", lambda h: h.tensor_copy(out=qTb[:, :, t * 128:(t + 1) * 128],
                                                  in_=psbs[pi][:, 0:512].rearrange("p (s c) -> p s c", c=128)),
                   writes=[TqTb, PBB[pi]])
                yield

            def halo_chain(j, b):
                S.dma(xts[0][0:16, :], xh[j, b, :, :], Txts[0], writes=[Txts[0]])
                yield
                yield from norm_a(xts[0], Txts[0], 16, xn[0], Txn[0])
                yield from norm_b(16, j, 0, 1, lambda k: hTh[:, k, :], ThTh, xn[0], Txn[0], 1)

            ooff = 0
            for j, (SQj, SOj) in enumerate(jobs):
                NTO = SQj // 128
                NKT = (SQj + SOj) // 128
                build_grow(j, 2)
                def kv_src(kt):
                    if kt < NTO:
                        return xq[j, kt * 128:(kt + 1) * 128, :], ropeq[j, kt * 128:(kt + 1) * 128, :]
                    o = ooff + (kt - NTO) * 128
                    return xo[o:o + 128, :], ropeo[o:o + 128, :]

                def kv_gens():
                    for kt in range(NKT):
                        src, rsrc = kv_src(kt)
                        nsrc = kv_src(kt + 4)[0] if kt + 4 < NKT else None
                        yield kv_tile(j, kt, src, rsrc, kt % 4, nsrc)
                S.barrier()
                run_interleaved(kv_gens(), 4)
                S.barrier()

                NBLK = SQj // 512

                def pphase(j, b, qset, wait_flag=None, part="all"):
                    r0 = b * 512
                    pT = pTs[qset]; TpT = TpTs[qset]
                    if part in ("all", "q"):
                        S.dma(ropet[:], ropeq[j, r0:r0 + 512, :].rearrange("(t p) c -> p t c", p=128), Tropet, writes=[Tropet])
                        for t0 in range(2):
                            S.dma(xts[t0][:], xq[j, r0 + t0 * 128:r0 + (t0 + 1) * 128, :], Txts[t0], writes=[Txts[t0]])
                        yield
                        yield
                        yield from interleave_gen((q_tile(j, r0, t, t % 2, qset) for t in range(4)), 2)
                    if part == "q":
                        return
                    yield from halo_chain(j, b)
                    for g in range(4):
                        for k in range(8):
                            op("pe", lambda h, k=k, g=g: h.matmul(bank(6)[:, :], lhsT=w_in[:, k, 768 + g * 128:768 + (g + 1) * 128],
                                                                  rhs=hTb[:, k, :], start=(k == 0), stop=(k == 7)),
                               reads=ThTb + [Twin], writes=[PB[6]])
                        yield
                        yield
                        op("dve", lambda h, g=g: h.tensor_copy(out=uT[:, g, 8:520], in_=bank(6)[:, :]),
                           writes=[TuT, PB[6]])
                        yield
                    for g in range(4):
                        for k in range(8):
                            op("pe", lambda h, k=k, g=g: h.matmul(bank(6)[:, g * 16:(g + 1) * 16],
                                                                  lhsT=w_in[:, k, 768 + g * 128:768 + (g + 1) * 128],
                                                                  rhs=hTh[:, k, :], start=(k == 0), stop=(k == 7)),
                               reads=[ThTh, Twin], writes=[PB[6]])
                    yield
                    hv = bank(6)[:, 0:64].rearrange("p (g e) -> p g e", e=16)
                    mo = (j * NB + b) * 16
                    op("dve", lambda h: h.tensor_tensor(out=uT[:, :, 0:8], in0=hv[:, :, 0:8],
                                                        in1=hmask[:, mo:mo + 8].unsqueeze(1).broadcast_to([128, 4, 8]), op=ALU.mult),
                       reads=[Thmask], writes=[TuT, PB[6]])
                    op("dve", lambda h: h.tensor_tensor(out=uT[:, :, 520:528], in0=hv[:, :, 8:16],
                                                        in1=hmask[:, mo + 8:mo + 16].unsqueeze(1).broadcast_to([128, 4, 8]), op=ALU.mult),
                       reads=[Thmask], writes=[TuT, PB[6]])
                    yield
                    tt = lambda h, o, a, b_: h.tensor_tensor(out=o, in0=a, in1=b_, op=ALU.add)
                    E = lambda g: uT[:, g, :]
                    op("pool", lambda h: tt(h, sC[:, 0:527], E(3)[:, 0:527], E(3)[:, 1:528]), reads=[TuT], writes=[TsC])
                    op("dve", lambda h: tt(h, Wt[:, 0, :], E(0)[:, 7:519], E(0)[:, 8:520]), reads=[TuT], writes=[TWt[0]])
                    yield
                    op("dve", lambda h: tt(h, sA[:, 0:527], E(1)[:, 0:527], E(1)[:, 1:528]), reads=[TuT], writes=[TsA])
                    op("pool", lambda h: tt(h, sD_[:, 0:525], sC[:, 0:525], sC[:, 2:527]), reads=[TsC], writes=[TsD])
                    yield
                    op("dve", lambda h: tt(h, Wt[:, 1, :], sA[:, 6:518], sA[:, 8:520]), reads=[TsA], writes=[TWt[1]])
                    yield
                    op("dve", lambda h: tt(h, sB[:, 0:527], E(2)[:, 0:527], E(2)[:, 1:528]), reads=[TuT], writes=[TsB])
                    op("pool", lambda h: tt(h, sC[:, 0:521], sD_[:, 0:521], sD_[:, 4:525]), reads=[TsD], writes=[TsC])
                    yield
                    op("dve", lambda h: tt(h, sA[:, 0:525], sB[:, 0:525], sB[:, 2:527]), reads=[TsB], writes=[TsA])
                    yield
                    op("dve", lambda h: tt(h, Wt[:, 2, :], sA[:, 4:516], sA[:, 8:520]), reads=[TsA], writes=[TWt[2]])
                    op("pool", lambda h: tt(h, Wt[:, 3, :], sC[:, 0:512], sC[:, 8:520]), reads=[TsC], writes=[TWt[3]])
                    yield
                    for g in range(4):
                        op("dve", lambda h, g=g: h.scalar_tensor_tensor(out=mixT[:, g, :], in0=Wt[:, g, :], scalar=1.0 / POOL_W[g],
                                                                        in1=uT[:, g, 8:520], op0=ALU.mult, op1=ALU.subtract),
                           reads=[TWt[g], TuT], writes=[TmixT])
                        yield
                    for edge, do in ((0, b == 0), (1, b == NBLK - 1)):
                        if not do:
                            continue
                        io = (j * 2 + edge) * 32
                        c0 = 0 if edge == 0 else 504
                        op("dve", lambda h, io=io, c0=c0: h.tensor_tensor(
                            out=tmp8[:], in0=Wt[:, :, c0:c0 + 8],
                            in1=icnt[:, io:io + 32].rearrange("p (g e) -> p g e", e=8), op=ALU.mult),
                           reads=TWt + [Ticnt], writes=[Ttmp8])
                        op("dve", lambda h, c0=c0: h.tensor_tensor(out=mixT[:, :, c0:c0 + 8], in0=tmp8[:],
                                                                   in1=uT[:, :, 8 + c0:16 + c0], op=ALU.subtract),
                           reads=[Ttmp8, TuT], writes=[TmixT])
                        yield
                    while wait_flag is not None and not wait_flag[0]:
                        yield
                    for g in range(4):
                        op("pe", lambda h, g=g: h.matmul(bank(6)[:, :], lhsT=w_pool[:, g, :], rhs=mixT[:, g, :],
                                                         start=True, stop=True),
                           reads=[TmixT, Twpool], writes=[PB[6]])
                        yield
                        yield
                        op("dve", lambda h, g=g: h.tensor_scalar_mul(out=pT[:, g, :], in0=bank(6)[:, :], scalar1=pscale[:, g:g + 1]),
                           reads=[Tpscale], writes=[TpT, PB[6]])
                        yield

                def attn(j, b, qset):
                    qTb = qTbs[qset]; TqTb = TqTbs[qset]
                    aTb = aTs[b % 2]; TaT = TaTs[b % 2]
                    lo = slice(0, 64)
                    hi = slice(64, 128)
                    groups = [(s, jc) for s in range(4) for jc in range(NKT)]
                    NGR = len(groups)

                    def qk_group(i):
                        s, jc = groups[i]
                        sb_i = i % 2
                        op("pe", lambda h: h.matmul(bank(sb_i * 2)[:, :], lhsT=kT[lo, jc * 128:(jc + 1) * 128],
                                                    rhs=qTb[lo, s, :], start=True, stop=True),
                           reads=[TkT, TqTb], writes=[PB[sb_i * 2]])
                        op("pe", lambda h: h.matmul(bank(sb_i * 2 + 1)[:, :], lhsT=kT[hi, jc * 128:(jc + 1) * 128],
                                                    rhs=qTb[hi, s, :], start=True, stop=True),
                           reads=[TkT, TqTb], writes=[PB[sb_i * 2 + 1]])

                    qk_group(0)
                    qk_group(1)

                    def queue_norm(s):
                        for q4 in range(4):
                            cs = slice(q4 * 128, (q4 + 1) * 128)
                            pending.append(lambda cs=cs: op("dve", lambda h: h.reciprocal(out=recip[lo, cs], in_=osb[0][hi, cs]),
                                                            reads=[Tosb[0]], writes=[Trecip]))
                            pending.append(lambda cs=cs, s=s: op("pool", lambda h: h.tensor_tensor(
                                out=aTb[lo, s, cs], in0=osb[0][lo, cs], in1=recip[lo, cs], op=ALU.mult),
                                reads=[Trecip, Tosb[0]], writes=[TaT]))
                            pending.append(lambda cs=cs: op("dve", lambda h: h.reciprocal(out=recip[hi, cs], in_=osb[1][lo, cs]),
                                                            reads=[Tosb[1]], writes=[Trecip]))
                            pending.append(lambda cs=cs, s=s: op("pool", lambda h: h.tensor_tensor(
                                out=aTb[hi, s, cs], in0=osb[1][hi, cs], in1=recip[hi, cs], op=ALU.mult),
                                reads=[Trecip, Tosb[1]], writes=[TaT]))

                    for i, (s, jc) in enumerate(groups):
                        sb_i = i % 2
                        pi = i % 3
                        op("act", lambda h: h.activation(out=pt[pi][:, :], in_=ps[:, sb_i * 1024:(sb_i + 1) * 1024],
                                                         func=AF.Exp, scale=0.125),
                           writes=[Tpt[pi], PB[sb_i * 2], PB[sb_i * 2 + 1]])
                        if i + 2 < NGR:
                            qk_group(i + 2)
                        op("pe", lambda h: h.matmul(bank(4)[:, :], lhsT=vaug[:, jc, 0, :], rhs=pt[pi][:, 0:512],
                                                    start=(jc == 0), stop=(jc == NKT - 1)),
                           reads=[Tvaug, Tpt[pi]], writes=[PB[4]])
                        op("pe", lambda h: h.matmul(bank(5)[:, :], lhsT=vaug[:, jc, 1, :], rhs=pt[pi][:, 512:1024],
                                                    start=(jc == 0), stop=(jc == NKT - 1)),
                           reads=[Tvaug, Tpt[pi]], writes=[PB[5]])
                        if pending:
                            pending.pop(0)()
                        yield
                        if jc != NKT - 1:
                            continue
                        while pending:
                            pending.pop(0)()
                        for hh in range(2):
                            op("dve", lambda h, hh=hh: h.tensor_copy(out=osb[hh][:], in_=bank(4 + hh)[:, :]),
                               writes=[Tosb[hh], PB[4 + hh]])
                        queue_norm(s)
                        yield

                def ophase(j, b, qset):
                    r0 = b * 512
                    pT = pTs[qset]; TpT = TpTs[qset]
                    aTb = aTs[b % 2]; TaT = TaTs[b % 2]
                    for t in range(4):
                        xb = t % 2
                        b0 = (t % 2) * 2
                        for c in range(8):
                            src_t = aTb if c < 4 else pT
                            Tsrc = TaT if c < 4 else TpT
                            for half in range(2):
                                op("pe", lambda h, c=c, half=half, t=t, b0=b0, src_t=src_t: h.matmul(
                                    bank(b0 + half)[:, :], lhsT=src_t[:, c % 4, t * 128:(t + 1) * 128],
                                    rhs=w_out[:, c, half * 512:(half + 1) * 512], start=(c == 0), stop=(c == 7)),
                                   reads=[Tsrc, Twout], writes=[PB[b0 + half]])
                        ti = (r0 + t * 128) // 128
                        resid_epilogue(b0, None, None, tmpo, Ttmpo,
                                       x1buf[j, r0 + t * 128:r0 + (t + 1) * 128, :], X1T[j][ti], add_x=False)

                def ophase_side(j, b, done):
                    r0 = b * 512
                    pT = pTs[b % 2]; TpT = TpTs[b % 2]
                    aTb = aTs[b % 2]; TaT = TaTs[b % 2]
                    p7 = psb7[:, :].bitcast(F32)
                    while pending:
                        pending.pop(0)()
                    for t in range(4):
                        sst, Ts = next_ss()
                        op("pool", lambda h: h.memset(sst[:, 0:2], 0.0), writes=[Ts])
                        yield
                        for half in range(2):
                            yield
                            for c in range(8):
                                src_t = aTb if c < 4 else pT
                                Tsrc = TaT if c < 4 else TpT
                                op("pe", lambda h, c=c, src_t=src_t: h.matmul(
                                    p7, lhsT=src_t[:, c % 4, t * 128:(t + 1) * 128],
                                    rhs=w_out[:, c, half * 512:(half + 1) * 512], start=(c == 0), stop=(c == 7)),
                                   reads=[Tsrc, Twout], writes=[PB[7]])
                            yield
                            op("act", lambda h: h.activation(out=tmpo[:, half * 512:(half + 1) * 512], in_=p7, func=AF.Square,
                                                             accum_out=sst[:, half:half + 1]),
                               writes=[Ts, Ttmpo, PB[7]])
                            yield
                            op("dve", lambda h: h.tensor_copy(out=tmpo[:, half * 512:(half + 1) * 512], in_=p7),
                               writes=[Ttmpo, PB[7]])
                            yield
                        op("dve", lambda h: h.tensor_tensor(out=sst[:, 0:1], in0=sst[:, 0:1], in1=sst[:, 1:2], op=ALU.add), writes=[Ts])
                        yield
                        rstd_from_ss(sst[:, 0:1], Ts, 128, 1.0 / D)
                        yield
                        op("dve", lambda h: h.scalar_tensor_tensor(out=tmpo, in0=tmpo, scalar=sst[:, 0:1], in1=grow[:],
                                                                   op0=ALU.mult, op1=ALU.mult),
                           reads=[Ts, Tgrow], writes=[Ttmpo])
                        yield
                        ti = (r0 + t * 128) // 128
                        S.dma(x1buf[j, r0 + t * 128:r0 + (t + 1) * 128, :], tmpo, Ttmpo, reads=[Ttmpo], writes=[X1T[j][ti]])
                        yield
                    done[0] = True

                pending = []
                run_interleaved([pphase(j, 0, 0)], 1)
                for b in range(NBLK):
                    gens = [attn(j, b, b % 2)]
                    done = [True]
                    side1 = []
                    side2 = []
                    if b >= 1:
                        done = [False]
                        side2.append(ophase_side(j, b - 1, done))
                    if b + 1 < NBLK:
                        side1.append(pphase(j, b + 1, (b + 1) % 2, None, "q"))
                        side2.append(pphase(j, b + 1, (b + 1) % 2, done, "rest"))
                    if side1 or side2:
                        gens.append(itertools.chain(*(side1 + [interleave_gen(side2, 2)])))
                    run_interleaved(gens, 2)
                while pending:
                    pending.pop(0)()
                ophase(j, NBLK - 1, (NBLK - 1) % 2)
                ooff += SOj
            S.barrier()

        with ExitStack() as st:
            w1 = sb("w1_bf", [128, 8, DFF], BF16, st); Tw1 = Trk()
            w2 = sb("w2_bf", [128, 32, D], BF16, st); Tw2 = Trk()
            with ExitStack() as st2:
                stg = [sb("mstg%d" % i, [128, 2048], F32, st2) for i in range(2)]; Tstg = [Trk(), Trk()]
                load_cast(lambda i: w1[:, i // 2, (i % 2) * 2048:(i % 2 + 1) * 2048],
                          lambda i: w_ff1_d[:, i // 2, (i % 2) * 2048:(i % 2 + 1) * 2048], 16, stg, Tstg, Tw1, 2048)
                load_cast(lambda i: w2[:, 2 * i:2 * i + 2, :].rearrange("p a n -> p (a n)"),
                          lambda i: w_ff2_d[:, 2 * i:2 * i + 2, :].rearrange("p a n -> p (a n)"), 16, stg, Tstg, Tw2, 2048)
                S.barrier()
            xts = [sb("mxt%d" % i, [128, D], F32, st) for i in range(4)]; Txts = [Trk() for _ in range(4)]
            dts = [sb("mdt%d" % i, [128, D], F32, st) for i in range(2)]; Tdts = [Trk(), Trk()]
            xn = [sb("mxn%d" % i, [128, D], BF16, st) for i in range(2)]; Txn = [Trk(), Trk()]
            h2T = sb("h2T", [128, 8, 256], BF16, st); Th2T = [Trk(), Trk()]
            gT = sb("gT", [128, 32, 256], BF16, st); TgT = Trk()
            r32 = [sb("r32_%d" % i, [128, 256], F32, st) for i in range(2)]; Tr32 = [Trk(), Trk()]
            tmpo = rep[:, :, :].rearrange("p a n -> p (a n)"); Ttmpo = Trep

            blocks = [(j, blk) for j, (SQj, SOj) in enumerate(jobs) for blk in range(SQj // 256)]
            xt_i = [0]
            xb_of = {}

            def mlp_norm_a(bi):
                j, blk = blocks[bi]
                r0 = blk * 256
                xbs = []
                gens = []
                for t in range(2):
                    xb = xt_i[0] % 4
                    xt_i[0] += 1
                    xbs.append(xb)
                    ti = (r0 + t * 128) // 128
                    S.dma(xts[xb][:], xq[j, r0 + t * 128:r0 + (t + 1) * 128, :], Txts[xb], writes=[Txts[xb]])
                    S.dma(dts[t][:], x1buf[j, r0 + t * 128:r0 + (t + 1) * 128, :], Tdts[t],
                          reads=[X1T[j][ti]], writes=[Tdts[t]])
                    op("dve", lambda h, xb=xb, t=t: h.tensor_tensor(out=xts[xb][:], in0=xts[xb][:], in1=dts[t][:], op=ALU.add),
                       reads=[Tdts[t]], writes=[Txts[xb]])
                    gens.append(norm_a(xts[xb], Txts[xb], 128, xn[t], Txn[t]))
                xb_of[bi] = xbs
                run_interleaved(gens, 2)

            def mlp_norm_b(bi):
                j, blk = blocks[bi]
                run_interleaved([norm_b(128, j, 3, 4, (lambda k, t=t: h2T[:, k, t * 128:(t + 1) * 128]), Th2T[t], xn[t], Txn[t], t)
                                 for t in range(2)], 2)

            yi = 0
            cur_j = -1
            mlp_norm_a(0)
            mlp_norm_b(0)
            for bi, (j, blk) in enumerate(blocks):
                r0 = blk * 256
                if j != cur_j:
                    build_grow(j, 5)
                    cur_j = j
                for fc in range(32):
                    bk = fc % 2
                    for k in range(8):
                        op("pe", lambda h, k=k, fc=fc, bk=bk: h.matmul(bank(bk)[:, 0:256], lhsT=w1[:, k, fc * 128:(fc + 1) * 128],
                                                                       rhs=h2T[:, k, :], start=(k == 0), stop=(k == 7)),
                           reads=[Tw1] + Th2T, writes=[PB[bk]])
                    op("act", lambda h, bk=bk: h.activation(out=r32[bk][:], in_=bank(bk)[:, 0:256], func=AF.Relu),
                       writes=[Tr32[bk], PB[bk]])
                    op("dve", lambda h, bk=bk, fc=fc: h.tensor_tensor(out=gT[:, fc, :], in0=r32[bk][:], in1=r32[bk][:], op=ALU.mult),
                       reads=[Tr32[bk]], writes=[TgT])
                if bi + 1 < len(blocks):
                    mlp_norm_a(bi + 1)
                for t in range(2):
                    b0 = 2 + (yi % 2) * 2
                    yi += 1
                    for fc in range(32):
                        for half in range(2):
                            op("pe", lambda h, fc=fc, half=half, t=t, b0=b0: h.matmul(
                                bank(b0 + half)[:, :], lhsT=gT[:, fc, t * 128:(t + 1) * 128],
                                rhs=w2[:, fc, half * 512:(half + 1) * 512], start=(fc == 0), stop=(fc == 31)),
                               reads=[TgT, Tw2], writes=[PB[b0 + half]])
                    xb = xb_of[bi][t]
                    resid_epilogue(b0, xts[xb], Txts[xb], tmpo, Ttmpo,
                                   yout[j, r0 + t * 128:r0 + (t + 1) * 128, :], Trk())
                    if t == 0 and bi + 1 < len(blocks):
                        mlp_norm_b(bi + 1)
        S.finish()
    return nc


def _rope_table(pos):
    pos = np.asarray(pos)
    row = (pos // GRID_W).astype(np.float32)
    col = (pos % GRID_W).astype(np.float32)
    nf = 16
    inv = (1.0 / (np.float32(ROPE_THETA) ** (np.arange(nf, dtype=np.float32) / np.float32(nf)))).astype(np.float32)
    ang = np.concatenate([row[:, None] * inv[None, :], col[:, None] * inv[None, :]], axis=-1).astype(np.float32)
    c = np.cos(ang).astype(np.float32)
    s = np.sin(ang).astype(np.float32)
    return np.concatenate([c, c, -s, s], axis=-1).astype(np.float32)


def _pm(v, k):
    return np.ascontiguousarray(np.asarray(v, np.float32).reshape(k, 128).T)


def prep_shared(w_ada, b_ada, g_pre_mix, g_post_mix, g_pre_mlp, g_post_mlp, w_in, g_q, g_k, w_pool,
                pool_scale, w_out, w_ff1, w_ff2):
    f = np.float32
    sh = {}
    sh["w_ada"] = np.ascontiguousarray(np.asarray(w_ada, f).reshape(8, 128, 6144).transpose(1, 0, 2))
    sh["b_adaT"] = _pm(b_ada, 48)
    sh["gvec"] = np.ascontiguousarray(np.concatenate([_pm(g_pre_mix, 8), _pm(g_post_mix, 8), _pm(g_pre_mlp, 8),
                                                      _pm(g_post_mlp, 8)], axis=1))
    gq = np.asarray(g_q, f)
    gk = np.asarray(g_k, f)
    sh["gqk"] = np.ascontiguousarray(np.broadcast_to(np.concatenate([np.tile(gq, 8), np.tile(gk, 2)])[None, :], (128, 640)))
    w_in = np.asarray(w_in, f)
    qperm = np.concatenate([np.concatenate([np.arange(s * 64, (s + 1) * 64), np.arange((s + 4) * 64, (s + 5) * 64)])
                            for s in range(4)])
    cols = np.concatenate([qperm, np.arange(512, DIN)])
    sh["w_in"] = np.ascontiguousarray(w_in[:, cols].reshape(8, 128, DIN).transpose(1, 0, 2))
    w_out = np.asarray(w_out, f)
    rows = np.concatenate([qperm, np.arange(512, D)])
    sh["w_out"] = np.ascontiguousarray(w_out[rows, :].reshape(8, 128, D).transpose(1, 0, 2))
    sh["w_pool"] = np.ascontiguousarray(np.asarray(w_pool, f).transpose(1, 0, 2))
    sh["pscale"] = _pm(pool_scale, 4)
    sh["w_ff1"] = np.ascontiguousarray(np.asarray(w_ff1, f).reshape(8, 128, DFF).transpose(1, 0, 2))
    sh["w_ff2"] = np.ascontiguousarray(np.asarray(w_ff2, f).reshape(32, 128, D).transpose(1, 0, 2))
    return sh


def prep_core(core_jobs, jobs):
    f = np.float32
    NJ = len(jobs)
    SQ = jobs[0][0]
    NB = SQ // 512
    SOT = sum(j[1] for j in jobs)
    SOA = max(SOT, 128)
    xq = np.empty((NJ, SQ, D), f)
    xo = np.zeros((SOA, D), f)
    xh = np.zeros((NJ, NB, 16, D), f)
    ropeq = np.empty((NJ, SQ, 128), f)
    ropeo = np.zeros((SOA, 128), f)
    hmask = np.zeros((NJ, NB, 16), f)
    icnt = np.zeros((NJ, 2, 4, 8), f)
    cT = np.empty((128, 8, NJ), f)
    ooff = 0
    for j, ((xs, c, o0), (SQj, SOj)) in enumerate(zip(core_jobs, jobs)):
        Sseq = xs.shape[0]
        assert Sseq == SQj + SOj
        xq[j] = xs[o0:o0 + SQj]
        ropeq[j] = _rope_table(np.arange(o0, o0 + SQj))
        if SOj:
            opos = np.concatenate([np.arange(0, o0), np.arange(o0 + SQj, Sseq)])
            xo[ooff:ooff + SOj] = xs[opos]
            ropeo[ooff:ooff + SOj] = _rope_table(opos)
            ooff += SOj
        for b in range(NB):
            for e in range(16):
                t = o0 + b * 512 - 8 + e if e < 8 else o0 + (b + 1) * 512 + (e - 8)
                if 0 <= t < Sseq:
                    xh[j, b, e] = xs[t]
                    hmask[j, b, e] = 1.0
        for edge in range(2):
            for g, w in enumerate(POOL_W):
                for e in range(8):
                    t = o0 + e if edge == 0 else o0 + SQj - 8 + e
                    lo = min(max(t - w // 2, 0), Sseq)
                    hi = min(max(t - w // 2 + w, 0), Sseq)
                    icnt[j, edge, g, e] = f(1.0) / f(hi - lo)
        cT[:, :, j] = np.asarray(c, f).reshape(8, 128).T
    m = {
        "xq": xq, "xo": xo, "xh": xh, "ropeq": ropeq, "ropeo": ropeo,
        "hmask": np.ascontiguousarray(np.broadcast_to(hmask.reshape(1, -1), (128, NJ * NB * 16))),
        "icnt": np.ascontiguousarray(np.broadcast_to(icnt.reshape(1, -1), (128, NJ * 2 * 32))),
        "cT": np.ascontiguousarray(cT.reshape(128, 8 * NJ)),
    }
    return m


_NC_CACHE = {}


def kernel(x_prompt, x_sample, c_prompt, c_sample, w_ada, b_ada, g_pre_mix, g_post_mix, g_pre_mlp, g_post_mlp,
           w_in, g_q, g_k, w_pool, pool_scale, w_out, w_ff1, w_ff2):
    f = np.float32
    x_prompt = np.asarray(x_prompt, f)
    x_sample = np.asarray(x_sample, f)
    c_prompt = np.asarray(c_prompt, f)
    c_sample = np.asarray(c_sample, f)
    B, Sp, _ = x_prompt.shape
    Bs, Ss, _ = x_sample.shape
    assert B == 2 * N_CORES and Bs * 2 == N_CORES and Ss == 2 * Sp
    jobs = [(Sp, 0), (Sp, 0), (Sp, Sp)]
    sh = prep_shared(w_ada[0], b_ada[0], g_pre_mix[0], g_post_mix[0], g_pre_mlp[0], g_post_mlp[0], w_in[0],
                     g_q[0], g_k[0], w_pool[0], pool_scale[0], w_out[0], w_ff1[0], w_ff2[0])
    in_maps = []
    for c in range(N_CORES):
        cj = [(x_prompt[2 * c], c_prompt[2 * c], 0), (x_prompt[2 * c + 1], c_prompt[2 * c + 1], 0),
              (x_sample[c // 2], c_sample[c // 2], (c % 2) * Sp)]
        m = prep_core(cj, jobs)
        m.update(sh)
        in_maps.append(m)
    key = tuple(jobs)
    if key not in _NC_CACHE:
        _NC_CACHE[key] = build_program(jobs)
    nc = _NC_CACHE[key]
    res = run_bass_kernel_spmd(nc, in_maps, core_ids=list(range(N_CORES)))
    y_prompt = np.empty((B, Sp, D), f)
    y_sample = np.empty((Bs, Ss, D), f)
    for c in range(N_CORES):
        y = np.asarray(res.results[c]["y"], f)
        y_prompt[2 * c] = y[0]
        y_prompt[2 * c + 1] = y[1]
        y_sample[c // 2, (c % 2) * Sp:(c % 2 + 1) * Sp] = y[2]
    return (y_prompt, y_sample)
```

```python
from contextlib import ExitStack
import itertools
import numpy as np
import concourse.bass as bass
import concourse.mybir as mybir
from concourse.bass_utils import run_bass_kernel_spmd

F32 = mybir.dt.float32
BF16 = mybir.dt.bfloat16
AF = mybir.ActivationFunctionType
ALU = mybir.AluOpType
AX = mybir.AxisListType

D = 1024
DIN = 1280
DFF = 4096
EPS = 1e-6
POOL_W = (2, 4, 8, 16)
GRID_W = 64
ROPE_THETA = 10000.0
N_CORES = 8


class Trk:
    __slots__ = ("w", "r", "sem", "cnt", "name")

    def __init__(self, name=""):
        self.w = None
        self.r = {}
        self.sem = None
        self.cnt = 0
        self.name = name


class Sched:
    def __init__(self, nc):
        self.nc = nc
        self.eng = {}
        for name, h in (("pe", nc.tensor), ("act", nc.scalar), ("dve", nc.vector),
                        ("pool", nc.gpsimd), ("sp", nc.sync)):
            self.eng[name] = dict(h=h, sem=nc.alloc_semaphore("s_" + name), cnt=0, waited={})
        self.dma_trks = []
        self.nsem = 0

    def _deps(self, reads, writes):
        deps = {}
        for t in reads:
            if t.w is not None:
                k, s, v = t.w
                if k not in deps or deps[k][1] < v:
                    deps[k] = (s, v)
        for t in writes:
            if t.w is not None:
                k, s, v = t.w
                if k not in deps or deps[k][1] < v:
                    deps[k] = (s, v)
            for k, (s, v) in t.r.items():
                if k not in deps or deps[k][1] < v:
                    deps[k] = (s, v)
        return deps

    def _wait(self, ename, deps):
        e = self.eng[ename]
        for k, (s, v) in deps.items():
            if k == "pe" and ename == "pe":
                continue
            if e["waited"].get(k, 0) < v:
                e["h"].wait_ge(s, v)
                e["waited"][k] = v

    def op(self, ename, fn, reads=(), writes=()):
        e = self.eng[ename]
        self._wait(ename, self._deps(reads, writes))
        inst = fn(e["h"])
        e["cnt"] += 1
        inst.then_inc(e["sem"], 1)
        rec = (ename, e["sem"], e["cnt"])
        for t in writes:
            t.w = rec
            t.r = {}
        for t in reads:
            t.r[ename] = (e["sem"], e["cnt"])
        return inst

    def dma(self, out, in_, strk, reads=(), writes=()):
        e = self.eng["sp"]
        self._wait("sp", self._deps(reads, writes))
        if strk.sem is None:
            strk.sem = self.nc.alloc_semaphore("d%d" % self.nsem)
            self.nsem += 1
            self.dma_trks.append(strk)
        inst = e["h"].dma_start(out=out, in_=in_)
        strk.cnt += 16
        inst.then_inc(strk.sem, 16)
        key = "d:%d" % id(strk)
        rec = (key, strk.sem, strk.cnt)
        for t in writes:
            t.w = rec
            t.r = {}
        for t in reads:
            t.r[key] = (strk.sem, strk.cnt)
        return inst

    def barrier(self):
        for en, e in self.eng.items():
            for on, o in self.eng.items():
                if on == en or on == "sp" or o["cnt"] == 0:
                    continue
                if e["waited"].get(on, 0) < o["cnt"]:
                    e["h"].wait_ge(o["sem"], o["cnt"])
                    e["waited"][on] = o["cnt"]
            for t in self.dma_trks:
                k = "d:%d" % id(t)
                if t.cnt and e["waited"].get(k, 0) < t.cnt:
                    e["h"].wait_ge(t.sem, t.cnt)
                    e["waited"][k] = t.cnt

    def finish(self):
        e = self.eng["sp"]
        for t in self.dma_trks:
            e["h"].wait_ge(t.sem, t.cnt)


def run_interleaved(gens, width):
    it = iter(gens)
    active = []
    more = True
    while True:
        while more and len(active) < width:
            try:
                active.append(next(it))
            except StopIteration:
                more = False
        if not active:
            break
        for g in list(active):
            try:
                next(g)
            except StopIteration:
                active.remove(g)


def interleave_gen(gens, width):
    it = iter(gens)
    active = []
    more = True
    while True:
        while more and len(active) < width:
            try:
                active.append(next(it))
            except StopIteration:
                more = False
        if not active:
            return
        for g in list(active):
            try:
                next(g)
            except StopIteration:
                active.remove(g)
        yield


def multi(g, k):
    while True:
        for _ in range(k):
            try:
                next(g)
            except StopIteration:
                return
        yield


def build_program(jobs):
    NJ = len(jobs)
    SQ = jobs[0][0]
    assert all(j[0] == SQ for j in jobs)
    NB = SQ // 512
    SOT = sum(j[1] for j in jobs)
    SOA = max(SOT, 128)
    NKT_MAX = max((j[0] + j[1]) // 128 for j in jobs)

    nc = bass.Bass("TRN2", target_bir_lowering=False)

    def din(name, shape):
        return nc.dram_tensor(name, shape, F32, kind="ExternalInput").ap()

    xq = din("xq", [NJ, SQ, D])
    xo = din("xo", [SOA, D])
    xh = din("xh", [NJ, NB, 16, D])
    ropeq = din("ropeq", [NJ, SQ, 128])
    ropeo = din("ropeo", [SOA, 128])
    hmask_d = din("hmask", [128, NJ * NB * 16])
    icnt_d = din("icnt", [128, NJ * 2 * 32])
    cT_d = din("cT", [128, 8 * NJ])
    w_ada_d = din("w_ada", [128, 8, 6144])
    b_adaT_d = din("b_adaT", [128, 48])
    gvec_d = din("gvec", [128, 32])
    gqk_d = din("gqk", [128, 640])
    w_in_d = din("w_in", [128, 8, DIN])
    w_out_d = din("w_out", [128, 8, D])
    w_pool_d = din("w_pool", [128, 4, 128])
    pscale_d = din("pscale", [128, 4])
    w_ff1_d = din("w_ff1", [128, 8, DFF])
    w_ff2_d = din("w_ff2", [128, 32, D])
    yout = nc.dram_tensor("y", [NJ, SQ, D], F32, kind="ExternalOutput").ap()
    x1buf = nc.dram_tensor("x1buf", [NJ, SQ, D], F32).ap()

    S = Sched(nc)
    op = S.op

    ps = nc.alloc_psum_tensor("ps", [128, 7 * 512], F32)
    psb7 = nc.alloc_psum_tensor("psb7", [128, 1024], BF16)
    psbs = [psb7[:, :]] + [ps[:, i * 512:(i + 1) * 512].bitcast(BF16) for i in (6, 5, 4, 3)]
    PB = [Trk("bank%d" % i) for i in range(8)]
    PBB = [PB[7], PB[6], PB[5], PB[4], PB[3]]

    def bank(i, n=512, parts=slice(0, 128)):
        return ps[parts, i * 512:i * 512 + n]

    with ExitStack() as glob:
        def sb(name, shape, dt=F32, stack=glob):
            return stack.enter_context(nc.sbuf_tensor("sb_" + name, shape, dt))

        ident = sb("ident", [128, 128], BF16); Tident = Trk()
        identf = sb("identf", [128, 128], F32); Tidentf = Trk()
        eps_t = sb("eps_t", [128, 1]); Teps = Trk()
        modD = sb("modD", [128, 6, 8, NJ]); TmodD = Trk()
        gqk = sb("gqk", [128, 640]); Tgqk = Trk()
        hmask = sb("hmask", [128, NJ * NB * 16]); Thmask = Trk()
        icnt = sb("icnt", [128, NJ * 2 * 32]); Ticnt = Trk()
        pscale = sb("pscale", [128, 4]); Tpscale = Trk()
        grow = sb("grow", [128, D]); Tgrow = Trk()
        rep = sb("rep", [128, 8, 128]); Trep = Trk()
        NSS = 8
        ss_t = [sb("ss%d" % i, [128, 2]) for i in range(NSS)]; Tss = [Trk() for _ in range(NSS)]
        ss_i = [0]

        for t_, tr_ in ((ident, Tident), (identf, Tidentf)):
            op("pool", lambda h, t_=t_: h.memset(t_[:], 0.0), writes=[tr_])
            op("pool", lambda h, t_=t_: h.affine_select(out=t_[:], in_=t_[:], pattern=[[-1, 128]],
                                                       compare_op=ALU.not_equal, fill=1.0, base=0,
                                                       channel_multiplier=1), writes=[tr_])
        op("pool", lambda h: h.memset(eps_t[:], EPS), writes=[Teps])
        S.dma(gqk[:], gqk_d[:, :], Tgqk, writes=[Tgqk])
        S.dma(hmask[:], hmask_d[:, :], Thmask, writes=[Thmask])
        S.dma(icnt[:], icnt_d[:, :], Ticnt, writes=[Ticnt])
        S.dma(pscale[:], pscale_d[:, :], Tpscale, writes=[Tpscale])

        with ExitStack() as st:
            wa = [sb("wa%d" % i, [128, 8, 1024], F32, st) for i in range(2)]; Twa = [Trk(), Trk()]
            cT = sb("cT", [128, 8 * NJ], F32, st); TcT = Trk()
            e_t = sb("e_t", [128, 8 * NJ], F32, st); Te = Trk()
            sT = sb("sT", [128, 8 * NJ], F32, st); TsT = Trk()
            b_adaT = sb("b_adaT", [128, 48], F32, st); Tb = Trk()
            gvec = sb("gvec", [128, 32], F32, st); Tg = Trk()
            modT = sb("modT", [128, 6, 8, NJ], F32, st); TmodT = Trk()
            S.dma(cT[:], cT_d[:, :], TcT, writes=[TcT])
            S.dma(b_adaT[:], b_adaT_d[:, :], Tb, writes=[Tb])
            S.dma(gvec[:], gvec_d[:, :], Tg, writes=[Tg])
            op("act", lambda h: h.activation(out=e_t[:], in_=cT[:], func=AF.Exp, scale=-1.0),
               reads=[TcT], writes=[Te])
            op("dve", lambda h: h.tensor_scalar_add(out=e_t[:], in0=e_t[:], scalar1=1.0), writes=[Te])
            op("dve", lambda h: h.reciprocal(out=e_t[:], in_=e_t[:]), writes=[Te])
            op("dve", lambda h: h.tensor_mul(out=sT[:], in0=cT[:], in1=e_t[:]), reads=[TcT, Te], writes=[TsT])
            sT3 = sT[:, :].rearrange("p (k j) -> p k j", j=NJ)
            for blk in range(6):
                b = blk % 2
                S.dma(wa[b][:], w_ada_d[:, :, blk * 1024:(blk + 1) * 1024], Twa[b], writes=[Twa[b]])
                for mc in range(8):
                    for k in range(8):
                        op("pe", lambda h, b=b, mc=mc, k=k: h.matmul(
                            bank(b)[:, mc * NJ:(mc + 1) * NJ], lhsT=wa[b][:, k, mc * 128:(mc + 1) * 128],
                            rhs=sT3[:, k, :], start=(k == 0), stop=(k == 7)),
                           reads=[Twa[b], TsT], writes=[PB[b]])
                op("dve", lambda h, b=b, blk=blk: h.tensor_tensor(
                    out=modT[:, blk, :, :], in0=bank(b)[:, 0:8 * NJ].rearrange("p (m j) -> p m j", j=NJ),
                    in1=b_adaT[:, blk * 8:(blk + 1) * 8].unsqueeze(2).broadcast_to([128, 8, NJ]), op=ALU.add),
                   reads=[Tb], writes=[TmodT, PB[b]])

            def gb(i):
                return gvec[:, i * 8:(i + 1) * 8].unsqueeze(2).broadcast_to([128, 8, NJ])
            op("dve", lambda h: h.scalar_tensor_tensor(out=modD[:, 0], in0=modT[:, 1], scalar=1.0, in1=gb(0),
                                                       op0=ALU.add, op1=ALU.mult), reads=[TmodT, Tg], writes=[TmodD])
            op("dve", lambda h: h.tensor_copy(out=modD[:, 1], in_=modT[:, 0]), reads=[TmodT], writes=[TmodD])
            op("dve", lambda h: h.tensor_tensor(out=modD[:, 2], in0=modT[:, 2], in1=gb(1), op=ALU.mult),
               reads=[TmodT, Tg], writes=[TmodD])
            op("dve", lambda h: h.scalar_tensor_tensor(out=modD[:, 3], in0=modT[:, 4], scalar=1.0, in1=gb(2),
                                                       op0=ALU.add, op1=ALU.mult), reads=[TmodT, Tg], writes=[TmodD])
            op("dve", lambda h: h.tensor_copy(out=modD[:, 4], in_=modT[:, 3]), reads=[TmodT], writes=[TmodD])
            op("dve", lambda h: h.tensor_tensor(out=modD[:, 5], in0=modT[:, 5], in1=gb(3), op=ALU.mult),
               reads=[TmodT, Tg], writes=[TmodD])
            S.barrier()

        def build_grow(j, which):
            op("dve", lambda h: h.tensor_copy(out=rep[:], in_=modD[:, which, :, j].unsqueeze(2).broadcast_to([128, 8, 128])),
               reads=[TmodD], writes=[Trep])
            for k in range(8):
                bk = 4 + k // 4
                op("pe", lambda h, k=k, bk=bk: h.transpose(out=bank(bk)[:, (k % 4) * 128:(k % 4 + 1) * 128],
                                                          in_=rep[:, k, :], identity=identf[:]),
                   reads=[Trep, Tidentf], writes=[PB[bk]])
            for half in range(2):
                op("act", lambda h, half=half: h.activation(out=grow[:, half * 512:(half + 1) * 512],
                                                            in_=bank(4 + half)[:, :], func=AF.Copy),
                   writes=[Tgrow, PB[4 + half]])

        def rstd_from_ss(ssv, Tssv, rows, inv_n):
            op("act", lambda h: h.activation(out=ssv, in_=ssv, func=AF.Ln, scale=inv_n, bias=eps_t[0:rows, 0:1]),
               reads=[Teps], writes=[Tssv])
            op("act", lambda h: h.activation(out=ssv, in_=ssv, func=AF.Exp, scale=-0.5), writes=[Tssv])

        def next_ss():
            i = ss_i[0] % NSS
            ss_i[0] += 1
            return ss_t[i], Tss[i]

        def norm_a(xt_, Txt, rows, xn, Txn, ss_ext=None):
            sst, Ts = ss_ext if ss_ext is not None else next_ss()
            op("pool", lambda h: h.memset(sst[0:rows, 0:1], 0.0), writes=[Ts])
            yield
            op("act", lambda h: h.activation(out=xn[0:rows, :], in_=xt_[0:rows, :], func=AF.Square,
                                             accum_out=sst[0:rows, 0:1]),
               reads=[Txt], writes=[Ts, Txn])
            yield
            rstd_from_ss(sst[0:rows, 0:1], Ts, rows, 1.0 / D)
            yield
            op("dve", lambda h: h.tensor_scalar_mul(out=xn[0:rows, :], in0=xt_[0:rows, :], scalar1=sst[0:rows, 0:1]),
               reads=[Txt, Ts], writes=[Txn])
            yield

        def norm_b(rows, j, gsi, shi, hT_dst, Thd, xn, Txn, pi, act_split=False):
            psb = psbs[pi]
            for k in range(8):
                op("pe", lambda h, k=k: h.transpose(out=psb[:, k * rows:(k + 1) * rows],
                                                    in_=xn[0:rows, k * 128:(k + 1) * 128],
                                                    identity=ident[0:rows, 0:rows]),
                   reads=[Txn, Tident], writes=[PBB[pi]])
            yield
            for k in range(8):
                if act_split and k % 2 == 1:
                    op("act", lambda h, k=k: h.activation(out=hT_dst(k), in_=psb[:, k * rows:(k + 1) * rows], func=AF.Identity,
                                                          scale=modD[:, gsi, k, j:j + 1], bias=modD[:, shi, k, j:j + 1]),
                       reads=[TmodD], writes=[Thd, PBB[pi]])
                else:
                    op("dve", lambda h, k=k: h.tensor_scalar(out=hT_dst(k), in0=psb[:, k * rows:(k + 1) * rows],
                                                             scalar1=modD[:, gsi, k, j:j + 1], scalar2=modD[:, shi, k, j:j + 1],
                                                             op0=ALU.mult, op1=ALU.add),
                       reads=[TmodD], writes=[Thd, PBB[pi]])
                if k % 2 == 1:
                    yield

        def resid_epilogue(b0, xt_, Txt, tmp, Ttmp, dst_ap, Tdst, add_x=True):
            sst, Ts = next_ss()
            mv = ps[:, b0 * 512:b0 * 512 + 1024]
            op("pool", lambda h: h.memset(sst[:, 0:1], 0.0), writes=[Ts])
            op("act", lambda h: h.activation(out=tmp, in_=mv, func=AF.Square, accum_out=sst[:, 0:1]),
               writes=[Ts, Ttmp, PB[b0], PB[b0 + 1]])
            rstd_from_ss(sst[:, 0:1], Ts, 128, 1.0 / D)
            op("dve", lambda h: h.scalar_tensor_tensor(out=tmp, in0=mv, scalar=sst[:, 0:1], in1=grow[:],
                                                       op0=ALU.mult, op1=ALU.mult),
               reads=[Ts, Tgrow], writes=[Ttmp, PB[b0], PB[b0 + 1]])
            if add_x:
                op("pool", lambda h: h.tensor_tensor(out=xt_[:], in0=xt_[:], in1=tmp, op=ALU.add),
                   reads=[Ttmp], writes=[Txt])
                S.dma(dst_ap, xt_[:], Txt, reads=[Txt], writes=[Tdst])
            else:
                S.dma(dst_ap, tmp, Ttmp, reads=[Ttmp], writes=[Tdst])

        def load_cast(dst_view_fn, src_view_fn, nchunks, stg, Tstg, Tdst, width):
            for i in range(nchunks):
                b = i % 2
                S.dma(stg[b][:, 0:width], src_view_fn(i), Tstg[b], writes=[Tstg[b]])
                eng = "pool" if i % 2 == 0 else "dve"
                op(eng, lambda h, i=i, b=b: h.tensor_copy(out=dst_view_fn(i), in_=stg[b][:, 0:width]),
                   reads=[Tstg[b]], writes=[Tdst])

        X1T = [[Trk() for _ in range(SQ // 128)] for _ in range(NJ)]
        with ExitStack() as st:
            w_in = sb("w_in_bf", [128, 8, DIN], BF16, st); Twin = Trk()
            w_out = sb("w_out_bf", [128, 8, D], BF16, st); Twout = Trk()
            w_pool = sb("w_pool_bf", [128, 4, 128], BF16, st); Twpool = Trk()
            kT = sb("kT", [128, NKT_MAX * 128], BF16, st); TkT = Trk()
            rstd_all = sb("rstd_all", [128, SQ // 128, 2], F32, st); Trstd = [Trk() for _ in range(SQ // 128)]
            vaug = sb("vaug", [128, NKT_MAX, 2, 128], BF16, st); Tvaug = Trk()
            with ExitStack() as st2:
                stg = [sb("stg%d" % i, [128, 2560], F32, st2) for i in range(2)]; Tstg = [Trk(), Trk()]
                load_cast(lambda i: w_in[:, 2 * i:2 * i + 2, :].rearrange("p a n -> p (a n)"),
                          lambda i: w_in_d[:, 2 * i:2 * i + 2, :].rearrange("p a n -> p (a n)"), 4, stg, Tstg, Twin, 2560)
                load_cast(lambda i: w_out[:, 2 * i:2 * i + 2, :].rearrange("p a n -> p (a n)"),
                          lambda i: w_out_d[:, 2 * i:2 * i + 2, :].rearrange("p a n -> p (a n)"), 4, stg, Tstg, Twout, 2048)
                load_cast(lambda i: w_pool[:, :, :].rearrange("p a n -> p (a n)"),
                          lambda i: w_pool_d[:, :, :].rearrange("p a n -> p (a n)"), 1, stg, Tstg, Twpool, 512)
                S.barrier()
            xts = [sb("xt%d" % i, [128, D], F32, st) for i in range(2)]; Txts = [Trk(), Trk()]
            xn = [sb("xn%d" % i, [128, D], BF16, st) for i in range(2)]; Txn = [Trk(), Trk()]
            hTt = [sb("hTt%d" % i, [128, 8, 128], BF16, st) for i in range(2)]; ThTt = [Trk(), Trk()]
            hTb = sb("hTb", [128, 8, 512], BF16, st); ThTb = [Trk() for _ in range(4)]
            hTh = sb("hTh", [128, 8, 16], BF16, st); ThTh = Trk()
            ropet = sb("ropet", [128, 4, 128], F32, st); Tropet = Trk()
            ropek = [sb("ropek%d" % i, [128, 128], F32, st) for i in range(2)]; Tropek = [Trk(), Trk()]
            qtmp0 = sb("qtmp0", [128, 4, 512], F32, st); Tqtmp0 = [Trk() for _ in range(4)]
            Wt = sb("Wt", [128, 4, 512], F32, st); TWt = [Trk() for _ in range(4)]
            ktmp = [sb("ktmp%d" % i, [128, 4, 128], F32, st) for i in range(2)]; Tktmp = [[Trk() for _ in range(4)] for _ in range(2)]
            qss = [sb("qss%d" % i, [128, 8], F32, st) for i in range(2)]; Tqss = [Trk(), Trk()]
            qr = [sb("qr%d" % i, [128, 512], BF16, st) for i in range(2)]; Tqr = [Trk(), Trk()]
            qTbs = [sb("qTb%d" % i, [128, 4, 512], BF16, st) for i in range(2)]; TqTbs = [Trk(), Trk()]
            uT = sb("uT", [128, 4, 528], F32, st); TuT = Trk()
            sA = sb("sA", [128, 528], F32, st); TsA = Trk()
            sB = sb("sB", [128, 528], F32, st); TsB = Trk()
            sC = sb("sC", [128, 528], F32, st); TsC = Trk()
            sD_ = sb("sD", [128, 528], F32, st); TsD = Trk()
            tmp8 = sb("tmp8", [128, 4, 8], F32, st); Ttmp8 = Trk()
            mixT = sb("mixT", [128, 4, 512], BF16, st); TmixT = Trk()
            aTs = [sb("aT%d" % i, [128, 4, 512], BF16, st) for i in range(2)]; TaTs = [Trk(), Trk()]
            aT = aTs[0]
            pTs = [sb("pT%d" % i, [128, 4, 512], BF16, st) for i in range(2)]; TpTs = [Trk(), Trk()]
            pt = [sb("pt%d" % i, [128, 1024], BF16, st) for i in range(3)]; Tpt = [Trk() for _ in range(3)]
            recip = sb("recip", [128, 512], F32, st); Trecip = Trk()
            osb = [sb("osb%d" % i, [128, 512], F32, st) for i in range(2)]; Tosb = [Trk(), Trk()]
            tmpo = rep[:, :, :].rearrange("p a n -> p (a n)"); Ttmpo = Trep
            op("pool", lambda h: h.memset(vaug[:], 1.0), writes=[Tvaug])
            kr_x = [sb("krx%d" % i, [128, 128], BF16, st) for i in range(3)]
            qss_x = [sb("qssx%d" % i, [128, 8], F32, st) for i in range(3)]
            K_xts = xts + [Wt[:, 0:2, :].rearrange("p a n -> p (a n)"), Wt[:, 2:4, :].rearrange("p a n -> p (a n)"),
                           qtmp0[:, 0:2, :].rearrange("p a n -> p (a n)")]
            K_Txts = Txts + [Trk(), Trk(), Trk()]
            K_xn = xn + [aT[:, 0:2, :].rearrange("p a n -> p (a n)"), aT[:, 2:4, :].rearrange("p a n -> p (a n)"),
                         aTs[1][:, 0:2, :].rearrange("p a n -> p (a n)")]
            K_Txn = Txn + [Trk(), Trk(), Trk()]
            K_hTt = hTt + [mixT[:, 0:2, :].rearrange("p a (k c) -> p (a k) c", c=128),
                           mixT[:, 2:4, :].rearrange("p a (k c) -> p (a k) c", c=128)]
            K_hTt = K_hTt + [pTs[0][:, 0:2, :].rearrange("p a (k c) -> p (a k) c", c=128)]
            K_ThTt = ThTt + [Trk(), Trk(), Trk()]
            K_ktmp = ktmp + [sA[:, 0:512].rearrange("p (i c) -> p i c", c=128), sB[:, 0:512].rearrange("p (i c) -> p i c", c=128),
                               qtmp0[:, 2, :].rearrange("p (i c) -> p i c", c=128)]
            K_Tktmp = Tktmp + [[Trk() for _ in range(4)] for _ in range(3)]
            K_ropek = ropek + [sC[:, 0:128], sD_[:, 0:128], qtmp0[:, 3, 0:128]]
            K_Tropek = Tropek + [Trk(), Trk(), Trk()]
            K_qss = qss + qss_x
            K_Tqss = Tqss + [Trk(), Trk(), Trk()]
            K_qr = [qr[0][:, 0:128], qr[1][:, 0:128], kr_x[0][:, :], kr_x[1][:, :], kr_x[2][:, :]]
            K_Tqr = Tqr + [Trk(), Trk(), Trk()]

            def qk_process(psv, Tbank, nh, gsl, ropeC, ropeS, Trope, out_bf, Tout, tmps, Ttmps, qs, Tqs):
                n = nh * 64
                v3 = lambda a: a.rearrange("p (h d) -> p h d", d=64)
                qsq, qxn, qt1, qt2 = tmps
                Tqsq, Tqxn, Tqt1, Tqt2 = Ttmps
                op("act", lambda h: h.activation(out=qsq, in_=psv, func=AF.Square), writes=[Tqsq, Tbank])
                yield
                op("dve", lambda h: h.tensor_reduce(out=qs[:, 0:nh], in_=v3(qsq), axis=AX.X, op=ALU.add),
                   reads=[Tqsq], writes=[Tqs])
                yield
                rstd_from_ss(qs[:, 0:nh], Tqs, 128, 1.0 / 64)
                yield
                op("dve", lambda h: h.tensor_tensor(out=v3(qxn), in0=v3(psv),
                                                    in1=qs[:, 0:nh].unsqueeze(2).broadcast_to([128, nh, 64]), op=ALU.mult),
                   reads=[Tqs], writes=[Tqxn, Tbank])
                yield
                op("pool", lambda h: h.tensor_tensor(out=qxn, in0=qxn, in1=gsl, op=ALU.mult),
                   reads=[Tgqk], writes=[Tqxn])
                yield
                x3 = v3(qxn)
                op("dve", lambda h: h.tensor_tensor(out=v3(qt1), in0=x3,
                                                    in1=ropeC.unsqueeze(1).broadcast_to([128, nh, 64]), op=ALU.mult),
                   reads=[Tqxn, Trope], writes=[Tqt1])
                t23 = v3(qt2)
                op("pool", lambda h: h.tensor_tensor(out=t23[:, :, 0:32], in0=x3[:, :, 32:64],
                                                     in1=ropeS[:, 0:32].unsqueeze(1).broadcast_to([128, nh, 32]), op=ALU.mult),
                   reads=[Tqxn, Trope], writes=[Tqt2])
                yield
                op("pool", lambda h: h.tensor_tensor(out=t23[:, :, 32:64], in0=x3[:, :, 0:32],
                                                     in1=ropeS[:, 32:64].unsqueeze(1).broadcast_to([128, nh, 32]), op=ALU.mult),
                   reads=[Tqxn, Trope], writes=[Tqt2])
                yield
                op("dve", lambda h: h.tensor_tensor(out=out_bf, in0=qt1, in1=qt2, op=ALU.add),
                   reads=[Tqt1, Tqt2], writes=[Tout])
                yield

            def kv_tile(j, kt, src, rsrc, si):
                kvb = bank(si // 2)[:, (si % 2) * 256:(si % 2) * 256 + 256]
                S.dma(K_xts[si][:], src, K_Txts[si], writes=[K_Txts[si]])
                S.dma(K_ropek[si][:], rsrc, K_Tropek[si], writes=[K_Tropek[si]])
                yield
                yield from norm_a(K_xts[si], K_Txts[si], 128, K_xn[si], K_Txn[si],
                                  (rstd_all[:, kt, :], Trstd[kt]) if kt < SQ // 128 else None)
                yield from norm_b(128, j, 0, 1, lambda k: K_hTt[si][:, k, :], K_ThTt[si], K_xn[si], K_Txn[si], si, act_split=True)
                for k in range(8):
                    op("pe", lambda h, k=k: h.matmul(kvb[:, 0:256], lhsT=K_hTt[si][:, k, :], rhs=w_in[:, k, 512:768],
                                                     start=(k == 0), stop=(k == 7)),
                       reads=[K_ThTt[si], Twin], writes=[PB[si // 2]])
                yield
                op("act", lambda h: h.activation(out=vaug[:, kt, 0, 0:64], in_=kvb[:, 128:192], func=AF.Copy),
                   writes=[Tvaug, PB[si // 2]])
                op("act", lambda h: h.activation(out=vaug[:, kt, 1, 64:128], in_=kvb[:, 192:256], func=AF.Copy),
                   writes=[Tvaug, PB[si // 2]])
                yield
                yield from qk_process(kvb[:, 0:128], PB[si // 2], 2, gqk[:, 512:640], K_ropek[si][:, 0:64], K_ropek[si][:, 64:128],
                                      K_Tropek[si], K_qr[si], K_Tqr[si],
                                      [K_ktmp[si][:, i, :] for i in range(4)], K_Tktmp[si], K_qss[si], K_Tqss[si])
                op("pe", lambda h: h.transpose(out=psbs[si][:, 0:128], in_=K_qr[si], identity=ident[:]),
                   reads=[K_Tqr[si], Tident], writes=[PBB[si]])
                yield
                op("dve", lambda h: h.tensor_copy(out=kT[:, kt * 128:(kt + 1) * 128], in_=psbs[si][:, 0:128]),
                   writes=[TkT, PBB[si]])
                yield

            def q_tile(j, r0, t, si, qset):
                qTb = qTbs[qset]; TqTb = TqTbs[qset]
                pi = 1 - si
                fb = bank(6)[:, :] if si == 0 else psb7[:, :].bitcast(F32)
                Tfb = PB[6] if si == 0 else PB[7]
                ti_ = (r0 + t * 128) // 128
                op("dve", lambda h: h.tensor_scalar_mul(out=xn[si][:, :], in0=xts[si][:, :], scalar1=rstd_all[:, ti_, 0:1]),
                   reads=[Txts[si], Trstd[ti_]], writes=[Txn[si]])
                yield
                if t + 2 < 4:
                    S.dma(xts[si][:], xq[j, r0 + (t + 2) * 128:r0 + (t + 3) * 128, :], Txts[si], writes=[Txts[si]])
                yield
                yield from norm_b(128, j, 0, 1, lambda k: hTb[:, k, t * 128:(t + 1) * 128], ThTb[t], xn[si], Txn[si], pi)
                yield
                for k in range(8):
                    op("pe", lambda h, k=k: h.matmul(fb, lhsT=hTb[:, k, t * 128:(t + 1) * 128],
                                                     rhs=w_in[:, k, 0:512], start=(k == 0), stop=(k == 7)),
                       reads=[ThTb[t], Twin], writes=[Tfb])
                yield
                if si == 0:
                    tm = [qtmp0[:, i, :] for i in range(4)]; Ttm = Tqtmp0
                else:
                    tm = [Wt[:, i, :] for i in range(4)]; Ttm = TWt
                yield from qk_process(fb, Tfb, 8, gqk[:, 0:512], ropet[:, t, 0:64], ropet[:, t, 64:128], Tropet,
                                      qr[si][:, :], Tqr[si], tm, Ttm, qss[si], Tqss[si])
                yield
                for s in range(4):
                    op("pe", lambda h, s=s: h.transpose(out=psbs[pi][:, s * 128:(s + 1) * 128], in_=qr[si][:, s * 128:(s + 1) * 128],
                                                        identity=ident[:]),
                       reads=[Tqr[si], Tident], writes=[PBB[pi]])
                yield
                op("dve", lambda h: h.tensor_copy(out=qTb[:, :, t * 128:(t + 1) * 128],
                                                  in_=psbs[pi][:, 0:512].rearrange("p (s c) -> p s c", c=128)),
                   writes=[TqTb, PBB[pi]])
                yield

            def halo_chain(j, b):
                S.dma(xts[0][0:16, :], xh[j, b, :, :], Txts[0], writes=[Txts[0]])
                yield
                yield from norm_a(xts[0], Txts[0], 16, xn[0], Txn[0])
                yield from norm_b(16, j, 0, 1, lambda k: hTh[:, k, :], ThTh, xn[0], Txn[0], 1)

            ooff = 0
            for j, (SQj, SOj) in enumerate(jobs):
                NTO = SQj // 128
                NKT = (SQj + SOj) // 128
                build_grow(j, 2)
                def kv_gens():
                    for kt in range(NKT):
                        if kt < NTO:
                            src = xq[j, kt * 128:(kt + 1) * 128, :]
                            rsrc = ropeq[j, kt * 128:(kt + 1) * 128, :]
                        else:
                            o = ooff + (kt - NTO) * 128
                            src = xo[o:o + 128, :]
                            rsrc = ropeo[o:o + 128, :]
                        yield kv_tile(j, kt, src, rsrc, kt % 5)
                S.barrier()
                run_interleaved(kv_gens(), 5)
                S.barrier()

                NBLK = SQj // 512

                def pphase(j, b, qset, wait_flag=None, part="all"):
                    r0 = b * 512
                    pT = pTs[qset]; TpT = TpTs[qset]
                    if part in ("all", "q"):
                        S.dma(ropet[:], ropeq[j, r0:r0 + 512, :].rearrange("(t p) c -> p t c", p=128), Tropet, writes=[Tropet])
                        for t0 in range(2):
                            S.dma(xts[t0][:], xq[j, r0 + t0 * 128:r0 + (t0 + 1) * 128, :], Txts[t0], writes=[Txts[t0]])
                        yield
                        yield
                        yield from interleave_gen((q_tile(j, r0, t, t % 2, qset) for t in range(4)), 2)
                    if part == "q":
                        return
                    yield from halo_chain(j, b)
                    for g in range(4):
                        for k in range(8):
                            op("pe", lambda h, k=k, g=g: h.matmul(bank(6)[:, :], lhsT=w_in[:, k, 768 + g * 128:768 + (g + 1) * 128],
                                                                  rhs=hTb[:, k, :], start=(k == 0), stop=(k == 7)),
                               reads=ThTb + [Twin], writes=[PB[6]])
                        yield
                        yield
                        op("dve", lambda h, g=g: h.tensor_copy(out=uT[:, g, 8:520], in_=bank(6)[:, :]),
                           writes=[TuT, PB[6]])
                        yield
                    for g in range(4):
                        for k in range(8):
                            op("pe", lambda h, k=k, g=g: h.matmul(bank(6)[:, g * 16:(g + 1) * 16],
                                                                  lhsT=w_in[:, k, 768 + g * 128:768 + (g + 1) * 128],
                                                                  rhs=hTh[:, k, :], start=(k == 0), stop=(k == 7)),
                               reads=[ThTh, Twin], writes=[PB[6]])
                    yield
                    hv = bank(6)[:, 0:64].rearrange("p (g e) -> p g e", e=16)
                    mo = (j * NB + b) * 16
                    op("dve", lambda h: h.tensor_tensor(out=uT[:, :, 0:8], in0=hv[:, :, 0:8],
                                                        in1=hmask[:, mo:mo + 8].unsqueeze(1).broadcast_to([128, 4, 8]), op=ALU.mult),
                       reads=[Thmask], writes=[TuT, PB[6]])
                    op("dve", lambda h: h.tensor_tensor(out=uT[:, :, 520:528], in0=hv[:, :, 8:16],
                                                        in1=hmask[:, mo + 8:mo + 16].unsqueeze(1).broadcast_to([128, 4, 8]), op=ALU.mult),
                       reads=[Thmask], writes=[TuT, PB[6]])
                    yield
                    tt = lambda h, o, a, b_: h.tensor_tensor(out=o, in0=a, in1=b_, op=ALU.add)
                    E = lambda g: uT[:, g, :]
                    op("pool", lambda h: tt(h, sC[:, 0:527], E(3)[:, 0:527], E(3)[:, 1:528]), reads=[TuT], writes=[TsC])
                    op("dve", lambda h: tt(h, Wt[:, 0, :], E(0)[:, 7:519], E(0)[:, 8:520]), reads=[TuT], writes=[TWt[0]])
                    yield
                    op("dve", lambda h: tt(h, sA[:, 0:527], E(1)[:, 0:527], E(1)[:, 1:528]), reads=[TuT], writes=[TsA])
                    op("pool", lambda h: tt(h, sD_[:, 0:525], sC[:, 0:525], sC[:, 2:527]), reads=[TsC], writes=[TsD])
                    yield
                    op("dve", lambda h: tt(h, Wt[:, 1, :], sA[:, 6:518], sA[:, 8:520]), reads=[TsA], writes=[TWt[1]])
                    yield
                    op("dve", lambda h: tt(h, sB[:, 0:527], E(2)[:, 0:527], E(2)[:, 1:528]), reads=[TuT], writes=[TsB])
                    op("pool", lambda h: tt(h, sC[:, 0:521], sD_[:, 0:521], sD_[:, 4:525]), reads=[TsD], writes=[TsC])
                    yield
                    op("dve", lambda h: tt(h, sA[:, 0:525], sB[:, 0:525], sB[:, 2:527]), reads=[TsB], writes=[TsA])
                    yield
                    op("dve", lambda h: tt(h, Wt[:, 2, :], sA[:, 4:516], sA[:, 8:520]), reads=[TsA], writes=[TWt[2]])
                    op("pool", lambda h: tt(h, Wt[:, 3, :], sC[:, 0:512], sC[:, 8:520]), reads=[TsC], writes=[TWt[3]])
                    yield
                    for g in range(4):
                        op("dve", lambda h, g=g: h.scalar_tensor_tensor(out=mixT[:, g, :], in0=Wt[:, g, :], scalar=1.0 / POOL_W[g],
                                                                        in1=uT[:, g, 8:520], op0=ALU.mult, op1=ALU.subtract),
                           reads=[TWt[g], TuT], writes=[TmixT])
                        yield
                    for edge, do in ((0, b == 0), (1, b == NBLK - 1)):
                        if not do:
                            continue
                        io = (j * 2 + edge) * 32
                        c0 = 0 if edge == 0 else 504
                        op("dve", lambda h, io=io, c0=c0: h.tensor_tensor(
                            out=tmp8[:], in0=Wt[:, :, c0:c0 + 8],
                            in1=icnt[:, io:io + 32].rearrange("p (g e) -> p g e", e=8), op=ALU.mult),
                           reads=TWt + [Ticnt], writes=[Ttmp8])
                        op("dve", lambda h, c0=c0: h.tensor_tensor(out=mixT[:, :, c0:c0 + 8], in0=tmp8[:],
                                                                   in1=uT[:, :, 8 + c0:16 + c0], op=ALU.subtract),
                           reads=[Ttmp8, TuT], writes=[TmixT])
                        yield
                    while wait_flag is not None and not wait_flag[0]:
                        yield
                    for g in range(4):
                        op("pe", lambda h, g=g: h.matmul(bank(6)[:, :], lhsT=w_pool[:, g, :], rhs=mixT[:, g, :],
                                                         start=True, stop=True),
                           reads=[TmixT, Twpool], writes=[PB[6]])
                        yield
                        yield
                        op("dve", lambda h, g=g: h.tensor_scalar_mul(out=pT[:, g, :], in0=bank(6)[:, :], scalar1=pscale[:, g:g + 1]),
                           reads=[Tpscale], writes=[TpT, PB[6]])
                        yield

                def attn(j, b, qset):
                    qTb = qTbs[qset]; TqTb = TqTbs[qset]
                    aTb = aTs[b % 2]; TaT = TaTs[b % 2]
                    lo = slice(0, 64)
                    hi = slice(64, 128)
                    groups = [(s, jc) for s in range(4) for jc in range(NKT)]
                    NGR = len(groups)

                    def qk_group(i):
                        s, jc = groups[i]
                        sb_i = i % 2
                        op("pe", lambda h: h.matmul(bank(sb_i * 2)[:, :], lhsT=kT[lo, jc * 128:(jc + 1) * 128],
                                                    rhs=qTb[lo, s, :], start=True, stop=True),
                           reads=[TkT, TqTb], writes=[PB[sb_i * 2]])
                        op("pe", lambda h: h.matmul(bank(sb_i * 2 + 1)[:, :], lhsT=kT[hi, jc * 128:(jc + 1) * 128],
                                                    rhs=qTb[hi, s, :], start=True, stop=True),
                           reads=[TkT, TqTb], writes=[PB[sb_i * 2 + 1]])

                    qk_group(0)
                    qk_group(1)

                    def queue_norm(s):
                        for q4 in range(4):
                            cs = slice(q4 * 128, (q4 + 1) * 128)
                            pending.append(lambda cs=cs: op("dve", lambda h: h.reciprocal(out=recip[lo, cs], in_=osb[0][hi, cs]),
                                                            reads=[Tosb[0]], writes=[Trecip]))
                            pending.append(lambda cs=cs, s=s: op("pool", lambda h: h.tensor_tensor(
                                out=aTb[lo, s, cs], in0=osb[0][lo, cs], in1=recip[lo, cs], op=ALU.mult),
                                reads=[Trecip, Tosb[0]], writes=[TaT]))
                            pending.append(lambda cs=cs: op("dve", lambda h: h.reciprocal(out=recip[hi, cs], in_=osb[1][lo, cs]),
                                                            reads=[Tosb[1]], writes=[Trecip]))
                            pending.append(lambda cs=cs, s=s: op("pool", lambda h: h.tensor_tensor(
                                out=aTb[hi, s, cs], in0=osb[1][hi, cs], in1=recip[hi, cs], op=ALU.mult),
                                reads=[Trecip, Tosb[1]], writes=[TaT]))

                    for i, (s, jc) in enumerate(groups):
                        sb_i = i % 2
                        pi = i % 3
                        op("act", lambda h: h.activation(out=pt[pi][:, :], in_=ps[:, sb_i * 1024:(sb_i + 1) * 1024],
                                                         func=AF.Exp, scale=0.125),
                           writes=[Tpt[pi], PB[sb_i * 2], PB[sb_i * 2 + 1]])
                        if i + 2 < NGR:
                            qk_group(i + 2)
                        op("pe", lambda h: h.matmul(bank(4)[:, :], lhsT=vaug[:, jc, 0, :], rhs=pt[pi][:, 0:512],
                                                    start=(jc == 0), stop=(jc == NKT - 1)),
                           reads=[Tvaug, Tpt[pi]], writes=[PB[4]])
                        op("pe", lambda h: h.matmul(bank(5)[:, :], lhsT=vaug[:, jc, 1, :], rhs=pt[pi][:, 512:1024],
                                                    start=(jc == 0), stop=(jc == NKT - 1)),
                           reads=[Tvaug, Tpt[pi]], writes=[PB[5]])
                        if pending:
                            pending.pop(0)()
                        yield
                        if jc != NKT - 1:
                            continue
                        while pending:
                            pending.pop(0)()
                        for hh in range(2):
                            op("dve", lambda h, hh=hh: h.tensor_copy(out=osb[hh][:], in_=bank(4 + hh)[:, :]),
                               writes=[Tosb[hh], PB[4 + hh]])
                        queue_norm(s)
                        yield

                def ophase(j, b, qset):
                    r0 = b * 512
                    pT = pTs[qset]; TpT = TpTs[qset]
                    aTb = aTs[b % 2]; TaT = TaTs[b % 2]
                    for t in range(4):
                        xb = t % 2
                        b0 = (t % 2) * 2
                        for c in range(8):
                            src_t = aTb if c < 4 else pT
                            Tsrc = TaT if c < 4 else TpT
                            for half in range(2):
                                op("pe", lambda h, c=c, half=half, t=t, b0=b0, src_t=src_t: h.matmul(
                                    bank(b0 + half)[:, :], lhsT=src_t[:, c % 4, t * 128:(t + 1) * 128],
                                    rhs=w_out[:, c, half * 512:(half + 1) * 512], start=(c == 0), stop=(c == 7)),
                                   reads=[Tsrc, Twout], writes=[PB[b0 + half]])
                        ti = (r0 + t * 128) // 128
                        resid_epilogue(b0, None, None, tmpo, Ttmpo,
                                       x1buf[j, r0 + t * 128:r0 + (t + 1) * 128, :], X1T[j][ti], add_x=False)

                def ophase_side(j, b, done):
                    r0 = b * 512
                    pT = pTs[b % 2]; TpT = TpTs[b % 2]
                    aTb = aTs[b % 2]; TaT = TaTs[b % 2]
                    p7 = psb7[:, :].bitcast(F32)
                    while pending:
                        pending.pop(0)()
                    for t in range(4):
                        sst, Ts = next_ss()
                        op("pool", lambda h: h.memset(sst[:, 0:2], 0.0), writes=[Ts])
                        yield
                        for half in range(2):
                            yield
                            for c in range(8):
                                src_t = aTb if c < 4 else pT
                                Tsrc = TaT if c < 4 else TpT
                                op("pe", lambda h, c=c, src_t=src_t: h.matmul(
                                    p7, lhsT=src_t[:, c % 4, t * 128:(t + 1) * 128],
                                    rhs=w_out[:, c, half * 512:(half + 1) * 512], start=(c == 0), stop=(c == 7)),
                                   reads=[Tsrc, Twout], writes=[PB[7]])
                            yield
                            op("act", lambda h: h.activation(out=tmpo[:, half * 512:(half + 1) * 512], in_=p7, func=AF.Square,
                                                             accum_out=sst[:, half:half + 1]),
                               writes=[Ts, Ttmpo, PB[7]])
                            yield
                            op("dve", lambda h: h.tensor_copy(out=tmpo[:, half * 512:(half + 1) * 512], in_=p7),
                               writes=[Ttmpo, PB[7]])
                            yield
                        op("dve", lambda h: h.tensor_tensor(out=sst[:, 0:1], in0=sst[:, 0:1], in1=sst[:, 1:2], op=ALU.add), writes=[Ts])
                        yield
                        rstd_from_ss(sst[:, 0:1], Ts, 128, 1.0 / D)
                        yield
                        op("dve", lambda h: h.scalar_tensor_tensor(out=tmpo, in0=tmpo, scalar=sst[:, 0:1], in1=grow[:],
                                                                   op0=ALU.mult, op1=ALU.mult),
                           reads=[Ts, Tgrow], writes=[Ttmpo])
                        yield
                        ti = (r0 + t * 128) // 128
                        S.dma(x1buf[j, r0 + t * 128:r0 + (t + 1) * 128, :], tmpo, Ttmpo, reads=[Ttmpo], writes=[X1T[j][ti]])
                        yield
                    done[0] = True

                pending = []
                run_interleaved([pphase(j, 0, 0)], 1)
                for b in range(NBLK):
                    gens = [attn(j, b, b % 2)]
                    done = [True]
                    side1 = []
                    side2 = []
                    if b >= 1:
                        done = [False]
                        side2.append(ophase_side(j, b - 1, done))
                    if b + 1 < NBLK:
                        side1.append(pphase(j, b + 1, (b + 1) % 2, None, "q"))
                        side2.append(pphase(j, b + 1, (b + 1) % 2, done, "rest"))
                    if side1 or side2:
                        gens.append(itertools.chain(*(side1 + [interleave_gen(side2, 2)])))
                    run_interleaved(gens, 2)
                while pending:
                    pending.pop(0)()
                ophase(j, NBLK - 1, (NBLK - 1) % 2)
                ooff += SOj
            S.barrier()

        with ExitStack() as st:
            w1 = sb("w1_bf", [128, 8, DFF], BF16, st); Tw1 = Trk()
            w2 = sb("w2_bf", [128, 32, D], BF16, st); Tw2 = Trk()
            with ExitStack() as st2:
                stg = [sb("mstg%d" % i, [128, 2048], F32, st2) for i in range(2)]; Tstg = [Trk(), Trk()]
                load_cast(lambda i: w1[:, i // 2, (i % 2) * 2048:(i % 2 + 1) * 2048],
                          lambda i: w_ff1_d[:, i // 2, (i % 2) * 2048:(i % 2 + 1) * 2048], 16, stg, Tstg, Tw1, 2048)
                load_cast(lambda i: w2[:, 2 * i:2 * i + 2, :].rearrange("p a n -> p (a n)"),
                          lambda i: w_ff2_d[:, 2 * i:2 * i + 2, :].rearrange("p a n -> p (a n)"), 16, stg, Tstg, Tw2, 2048)
                S.barrier()
            xts = [sb("mxt%d" % i, [128, D], F32, st) for i in range(4)]; Txts = [Trk() for _ in range(4)]
            dts = [sb("mdt%d" % i, [128, D], F32, st) for i in range(2)]; Tdts = [Trk(), Trk()]
            xn = [sb("mxn%d" % i, [128, D], BF16, st) for i in range(2)]; Txn = [Trk(), Trk()]
            h2T = sb("h2T", [128, 8, 256], BF16, st); Th2T = [Trk(), Trk()]
            gT = sb("gT", [128, 32, 256], BF16, st); TgT = Trk()
            r32 = [sb("r32_%d" % i, [128, 256], F32, st) for i in range(2)]; Tr32 = [Trk(), Trk()]
            tmpo = rep[:, :, :].rearrange("p a n -> p (a n)"); Ttmpo = Trep

            blocks = [(j, blk) for j, (SQj, SOj) in enumerate(jobs) for blk in range(SQj // 256)]
            xt_i = [0]
            xb_of = {}

            def mlp_norm_a(bi):
                j, blk = blocks[bi]
                r0 = blk * 256
                xbs = []
                gens = []
                for t in range(2):
                    xb = xt_i[0] % 4
                    xt_i[0] += 1
                    xbs.append(xb)
                    ti = (r0 + t * 128) // 128
                    S.dma(xts[xb][:], xq[j, r0 + t * 128:r0 + (t + 1) * 128, :], Txts[xb], writes=[Txts[xb]])
                    S.dma(dts[t][:], x1buf[j, r0 + t * 128:r0 + (t + 1) * 128, :], Tdts[t],
                          reads=[X1T[j][ti]], writes=[Tdts[t]])
                    op("dve", lambda h, xb=xb, t=t: h.tensor_tensor(out=xts[xb][:], in0=xts[xb][:], in1=dts[t][:], op=ALU.add),
                       reads=[Tdts[t]], writes=[Txts[xb]])
                    gens.append(norm_a(xts[xb], Txts[xb], 128, xn[t], Txn[t]))
                xb_of[bi] = xbs
                run_interleaved(gens, 2)

            def mlp_norm_b(bi):
                j, blk = blocks[bi]
                run_interleaved([norm_b(128, j, 3, 4, (lambda k, t=t: h2T[:, k, t * 128:(t + 1) * 128]), Th2T[t], xn[t], Txn[t], t)
                                 for t in range(2)], 2)

            yi = 0
            cur_j = -1
            mlp_norm_a(0)
            mlp_norm_b(0)
            for bi, (j, blk) in enumerate(blocks):
                r0 = blk * 256
                if j != cur_j:
                    build_grow(j, 5)
                    cur_j = j
                for fc in range(32):
                    bk = fc % 2
                    for k in range(8):
                        op("pe", lambda h, k=k, fc=fc, bk=bk: h.matmul(bank(bk)[:, 0:256], lhsT=w1[:, k, fc * 128:(fc + 1) * 128],
                                                                       rhs=h2T[:, k, :], start=(k == 0), stop=(k == 7)),
                           reads=[Tw1] + Th2T, writes=[PB[bk]])
                    op("act", lambda h, bk=bk: h.activation(out=r32[bk][:], in_=bank(bk)[:, 0:256], func=AF.Relu),
                       writes=[Tr32[bk], PB[bk]])
                    op("dve", lambda h, bk=bk, fc=fc: h.tensor_tensor(out=gT[:, fc, :], in0=r32[bk][:], in1=r32[bk][:], op=ALU.mult),
                       reads=[Tr32[bk]], writes=[TgT])
                if bi + 1 < len(blocks):
                    mlp_norm_a(bi + 1)
                for t in range(2):
                    b0 = 2 + (yi % 2) * 2
                    yi += 1
                    for fc in range(32):
                        for half in range(2):
                            op("pe", lambda h, fc=fc, half=half, t=t, b0=b0: h.matmul(
                                bank(b0 + half)[:, :], lhsT=gT[:, fc, t * 128:(t + 1) * 128],
                                rhs=w2[:, fc, half * 512:(half + 1) * 512], start=(fc == 0), stop=(fc == 31)),
                               reads=[TgT, Tw2], writes=[PB[b0 + half]])
                    xb = xb_of[bi][t]
                    resid_epilogue(b0, xts[xb], Txts[xb], tmpo, Ttmpo,
                                   yout[j, r0 + t * 128:r0 + (t + 1) * 128, :], Trk())
                    if t == 0 and bi + 1 < len(blocks):
                        mlp_norm_b(bi + 1)
        S.finish()
    return nc


def _rope_table(pos):
    pos = np.asarray(pos)
    row = (pos // GRID_W).astype(np.float32)
    col = (pos % GRID_W).astype(np.float32)
    nf = 16
    inv = (1.0 / (np.float32(ROPE_THETA) ** (np.arange(nf, dtype=np.float32) / np.float32(nf)))).astype(np.float32)
    ang = np.concatenate([row[:, None] * inv[None, :], col[:, None] * inv[None, :]], axis=-1).astype(np.float32)
    c = np.cos(ang).astype(np.float32)
    s = np.sin(ang).astype(np.float32)
    return np.concatenate([c, c, -s, s], axis=-1).astype(np.float32)


def _pm(v, k):
    return np.ascontiguousarray(np.asarray(v, np.float32).reshape(k, 128).T)


def prep_shared(w_ada, b_ada, g_pre_mix, g_post_mix, g_pre_mlp, g_post_mlp, w_in, g_q, g_k, w_pool,
                pool_scale, w_out, w_ff1, w_ff2):
    f = np.float32
    sh = {}
    sh["w_ada"] = np.ascontiguousarray(np.asarray(w_ada, f).reshape(8, 128, 6144).transpose(1, 0, 2))
    sh["b_adaT"] = _pm(b_ada, 48)
    sh["gvec"] = np.ascontiguousarray(np.concatenate([_pm(g_pre_mix, 8), _pm(g_post_mix, 8), _pm(g_pre_mlp, 8),
                                                      _pm(g_post_mlp, 8)], axis=1))
    gq = np.asarray(g_q, f)
    gk = np.asarray(g_k, f)
    sh["gqk"] = np.ascontiguousarray(np.broadcast_to(np.concatenate([np.tile(gq, 8), np.tile(gk, 2)])[None, :], (128, 640)))
    w_in = np.asarray(w_in, f)
    qperm = np.concatenate([np.concatenate([np.arange(s * 64, (s + 1) * 64), np.arange((s + 4) * 64, (s + 5) * 64)])
                            for s in range(4)])
    cols = np.concatenate([qperm, np.arange(512, DIN)])
    sh["w_in"] = np.ascontiguousarray(w_in[:, cols].reshape(8, 128, DIN).transpose(1, 0, 2))
    w_out = np.asarray(w_out, f)
    rows = np.concatenate([qperm, np.arange(512, D)])
    sh["w_out"] = np.ascontiguousarray(w_out[rows, :].reshape(8, 128, D).transpose(1, 0, 2))
    sh["w_pool"] = np.ascontiguousarray(np.asarray(w_pool, f).transpose(1, 0, 2))
    sh["pscale"] = _pm(pool_scale, 4)
    sh["w_ff1"] = np.ascontiguousarray(np.asarray(w_ff1, f).reshape(8, 128, DFF).transpose(1, 0, 2))
    sh["w_ff2"] = np.ascontiguousarray(np.asarray(w_ff2, f).reshape(32, 128, D).transpose(1, 0, 2))
    return sh


def prep_core(core_jobs, jobs):
    f = np.float32
    NJ = len(jobs)
    SQ = jobs[0][0]
    NB = SQ // 512
    SOT = sum(j[1] for j in jobs)
    SOA = max(SOT, 128)
    xq = np.empty((NJ, SQ, D), f)
    xo = np.zeros((SOA, D), f)
    xh = np.zeros((NJ, NB, 16, D), f)
    ropeq = np.empty((NJ, SQ, 128), f)
    ropeo = np.zeros((SOA, 128), f)
    hmask = np.zeros((NJ, NB, 16), f)
    icnt = np.zeros((NJ, 2, 4, 8), f)
    cT = np.empty((128, 8, NJ), f)
    ooff = 0
    for j, ((xs, c, o0), (SQj, SOj)) in enumerate(zip(core_jobs, jobs)):
        Sseq = xs.shape[0]
        assert Sseq == SQj + SOj
        xq[j] = xs[o0:o0 + SQj]
        ropeq[j] = _rope_table(np.arange(o0, o0 + SQj))
        if SOj:
            opos = np.concatenate([np.arange(0, o0), np.arange(o0 + SQj, Sseq)])
            xo[ooff:ooff + SOj] = xs[opos]
            ropeo[ooff:ooff + SOj] = _rope_table(opos)
            ooff += SOj
        for b in range(NB):
            for e in range(16):
                t = o0 + b * 512 - 8 + e if e < 8 else o0 + (b + 1) * 512 + (e - 8)
                if 0 <= t < Sseq:
                    xh[j, b, e] = xs[t]
                    hmask[j, b, e] = 1.0
        for edge in range(2):
            for g, w in enumerate(POOL_W):
                for e in range(8):
                    t = o0 + e if edge == 0 else o0 + SQj - 8 + e
                    lo = min(max(t - w // 2, 0), Sseq)
                    hi = min(max(t - w // 2 + w, 0), Sseq)
                    icnt[j, edge, g, e] = f(1.0) / f(hi - lo)
        cT[:, :, j] = np.asarray(c, f).reshape(8, 128).T
    m = {
        "xq": xq, "xo": xo, "xh": xh, "ropeq": ropeq, "ropeo": ropeo,
        "hmask": np.ascontiguousarray(np.broadcast_to(hmask.reshape(1, -1), (128, NJ * NB * 16))),
        "icnt": np.ascontiguousarray(np.broadcast_to(icnt.reshape(1, -1), (128, NJ * 2 * 32))),
        "cT": np.ascontiguousarray(cT.reshape(128, 8 * NJ)),
    }
    return m


_NC_CACHE = {}


def kernel(x_prompt, x_sample, c_prompt, c_sample, w_ada, b_ada, g_pre_mix, g_post_mix, g_pre_mlp, g_post_mlp,
           w_in, g_q, g_k, w_pool, pool_scale, w_out, w_ff1, w_ff2):
    f = np.float32
    x_prompt = np.asarray(x_prompt, f)
    x_sample = np.asarray(x_sample, f)
    c_prompt = np.asarray(c_prompt, f)
    c_sample = np.asarray(c_sample, f)
    B, Sp, _ = x_prompt.shape
    Bs, Ss, _ = x_sample.shape
    assert B == 2 * N_CORES and Bs * 2 == N_CORES and Ss == 2 * Sp
    jobs = [(Sp, 0), (Sp, 0), (Sp, Sp)]
    sh = prep_shared(w_ada[0], b_ada[0], g_pre_mix[0], g_post_mix[0], g_pre_mlp[0], g_post_mlp[0], w_in[0],
                     g_q[0], g_k[0], w_pool[0], pool_scale[0], w_out[0], w_ff1[0], w_ff2[0])
    in_maps = []
    for c in range(N_CORES):
        cj = [(x_prompt[2 * c], c_prompt[2 * c], 0), (x_prompt[2 * c + 1], c_prompt[2 * c + 1], 0),
              (x_sample[c // 2], c_sample[c // 2], (c % 2) * Sp)]
        m = prep_core(cj, jobs)
        m.update(sh)
        in_maps.append(m)
    key = tuple(jobs)
    if key not in _NC_CACHE:
        _NC_CACHE[key] = build_program(jobs)
    nc = _NC_CACHE[key]
    res = run_bass_kernel_spmd(nc, in_maps, core_ids=list(range(N_CORES)))
    y_prompt = np.empty((B, Sp, D), f)
    y_sample = np.empty((Bs, Ss, D), f)
    for c in range(N_CORES):
        y = np.asarray(res.results[c]["y"], f)
        y_prompt[2 * c] = y[0]
        y_prompt[2 * c + 1] = y[1]
        y_sample[c // 2, (c % 2) * Sp:(c % 2 + 1) * Sp] = y[2]
    return (y_prompt, y_sample)
```
